# Optimizing a Trainium2 kernel written in Bass

```python
import jax, jax.numpy as jnp
from jax import lax
import numpy as np

D_MODEL = 4096
BATCH = 1
SEQ = 16384
DEPTH = 4

GRID_W = 64
CTX_LEN = 256
D_MIX = D_MODEL
W_CONV = D_MIX // 4
W_REC = D_MIX // 4
W_ATT = D_MIX - W_CONV - W_REC
CONV_K = 31
REC_HEADS = 8
REC_BLOCK = W_REC // REC_HEADS
REC_CONV_K = 4
REC_C = 8.0
ATT_HEAD_DIM = 128
ATT_HEADS = W_ATT // ATT_HEAD_DIM
WIN_H = 8
WIN_W = 16
SPLIT_SIZES = (W_CONV, W_CONV, W_CONV, W_REC, W_REC, W_ATT, W_ATT, W_ATT, W_ATT)
N_IN = sum(SPLIT_SIZES)
EPS = 1e-6

kernel_name = "hybrid_conv_rglru_natten_prefix_dit"


def rmsnorm(x, g):
    xf = x.astype(jnp.float32)
    y = xf * lax.rsqrt(jnp.mean(xf * xf, axis=-1, keepdims=True) + EPS)
    return (y * g.astype(jnp.float32)).astype(x.dtype)


def depthwise_conv(x, w, b, pad):
    y = lax.conv_general_dilated(x, w[:, None, :].astype(x.dtype), (1,), [pad],
                                 dimension_numbers=('NWC', 'WIO', 'NWC'),
                                 feature_group_count=x.shape[-1])
    return y + b.astype(x.dtype)


def conformer_conv(val, glu_gate, w_dw, b_dw, ln_g, ln_b, w_pw, b_pw):
    u = val * jax.nn.sigmoid(glu_gate)
    u = depthwise_conv(u, w_dw, b_dw, (CONV_K // 2, CONV_K // 2))
    uf = u.astype(jnp.float32)
    mu = jnp.mean(uf, axis=-1, keepdims=True)
    var = jnp.mean(jnp.square(uf - mu), axis=-1, keepdims=True)
    uf = (uf - mu) * lax.rsqrt(var + EPS) * ln_g.astype(jnp.float32) + ln_b.astype(jnp.float32)
    u = jax.nn.silu(uf).astype(val.dtype)
    return u @ w_pw + b_pw


def linear_scan(a, b, h0):
    b = b.at[:, 0].add(a[:, 0] * h0)
    def combine(l, r):
        return (l[0] * r[0], r[0] * l[1] + r[1])
    _, h = lax.associative_scan(combine, (a, b), axis=1)
    return h


def rglru_direction(x, conv_w, conv_b, w_r, b_r, w_i, b_i, lam, h0, reverse):
    pad = (0, REC_CONV_K - 1) if reverse else (REC_CONV_K - 1, 0)
    xc = depthwise_conv(x, conv_w, conv_b, pad)
    bsz, s_len, _ = xc.shape
    xh = xc.reshape(bsz, s_len, REC_HEADS, REC_BLOCK)
    r = jax.nn.sigmoid((jnp.einsum('bshk,hkj->bshj', xh, w_r).reshape(bsz, s_len, W_REC) + b_r).astype(jnp.float32))
    i = jax.nn.sigmoid((jnp.einsum('bshk,hkj->bshj', xh, w_i).reshape(bsz, s_len, W_REC) + b_i).astype(jnp.float32))
    log_a = -REC_C * r * jax.nn.softplus(-lam.astype(jnp.float32))
    a = jnp.exp(log_a)
    gated = jnp.sqrt(-jnp.expm1(2.0 * log_a)) * (i * xc.astype(jnp.float32))
    if reverse:
        a, gated = a[:, ::-1], gated[:, ::-1]
    h = linear_scan(a, gated, h0)
    h_last = h[:, -1]
    if reverse:
        h = h[:, ::-1]
    return h.astype(x.dtype), h_last


def neighbourhood_attention(q, k, v, k_ctx, v_ctx, rpb):
    bsz, s_len, n_h, d_h = q.shape
    rows = s_len // GRID_W
    kh = min(WIN_H, rows)
    scale = d_h ** -0.5
    qg = q.reshape(bsz, rows, GRID_W, n_h, d_h)
    kg = k.reshape(bsz, rows, GRID_W, n_h, d_h)
    vg = v.reshape(bsz, rows, GRID_W, n_h, d_h)
    cols = np.arange(GRID_W)
    col_start = np.clip(cols - WIN_W // 2, 0, GRID_W - WIN_W)
    col_idx = col_start[:, None] + np.arange(WIN_W)[None, :]
    col_bias_idx = col_idx - cols[:, None] + (WIN_W - 1)
    n_loc = kh * WIN_W

    def one_row(r):
        rs = jnp.clip(r - WIN_H // 2, 0, rows - kh)
        q_r = lax.dynamic_index_in_dim(qg, r, axis=1, keepdims=False)
        k_rows = lax.dynamic_slice_in_dim(kg, rs, kh, axis=1)
        v_rows = lax.dynamic_slice_in_dim(vg, rs, kh, axis=1)
        k_win = k_rows[:, :, col_idx]
        v_win = v_rows[:, :, col_idx]
        s_loc = jnp.einsum('bqhd,bkqjhd->bhqkj', q_r, k_win).astype(jnp.float32) * scale
        row_bias_idx = rs + jnp.arange(kh) - r + (WIN_H - 1)
        bias = rpb[:, row_bias_idx[None, :, None], col_bias_idx[:, None, :]]
        s_loc = s_loc + bias.astype(jnp.float32)[None]
        s_ctx = jnp.einsum('bqhd,blhd->bhql', q_r, k_ctx).astype(jnp.float32) * scale
        s = jnp.concatenate([s_loc.reshape(bsz, n_h, GRID_W, n_loc), s_ctx], axis=-1)
        p = jax.nn.softmax(s, axis=-1).astype(q.dtype)
        p_loc = p[..., :n_loc].reshape(bsz, n_h, GRID_W, kh, WIN_W)
        p_ctx = p[..., n_loc:]
        return (jnp.einsum('bhqkj,bkqjhd->bqhd', p_loc, v_win)
                + jnp.einsum('bhql,blhd->bqhd', p_ctx, v_ctx))

    out = lax.map(one_row, jnp.arange(rows))
    return jnp.moveaxis(out, 0, 1).reshape(bsz, s_len, n_h * d_h)


def context_attention(q, k, v):
    bsz, l_len, n_h, d_h = q.shape
    s = jnp.einsum('bqhd,bkhd->bhqk', q, k).astype(jnp.float32) * (d_h ** -0.5)
    p = jax.nn.softmax(s, axis=-1).astype(q.dtype)
    return jnp.einsum('bhqk,bkhd->bqhd', p, v).reshape(bsz, l_len, n_h * d_h)


def split_proj(z):
    idx = [int(v) for v in np.cumsum(SPLIT_SIZES)[:-1]]
    return jnp.split(z, idx, axis=-1)


def hybrid_layer(xl, xc, c, c_ctx, w_ada, b_ada, norm_g, w_in, conv_w, conv_b, ln_g, ln_b,
                 w_pw, b_pw, rconv_w, rconv_b, w_r, b_r, w_i, b_i, lam, rpb, w_out, need_ctx):
    bsz = xl.shape[0]
    mod_l = jax.nn.silu(c) @ w_ada + b_ada
    mod_c = jax.nn.silu(c_ctx) @ w_ada + b_ada
    sh_l, sc_l, g_l = jnp.split(mod_l[:, None, :], 3, axis=-1)
    sh_c, sc_c, g_c = jnp.split(mod_c, 3)
    hl = rmsnorm(xl, norm_g) * (1.0 + sc_l) + sh_l
    hc = rmsnorm(xc, norm_g) * (1.0 + sc_c) + sh_c
    a_val_l, a_glu_l, a_gate_l, r_x_l, r_gate_l, q_l, k_l, v_l, c_gate_l = split_proj(hl @ w_in)
    a_val_c, a_glu_c, a_gate_c, r_x_c, r_gate_c, q_c, k_c, v_c, c_gate_c = split_proj(hc @ w_in)

    ya_l = conformer_conv(a_val_l, a_glu_l, conv_w, conv_b, ln_g, ln_b, w_pw, b_pw)

    yb_l = jnp.zeros_like(r_x_l)
    yb_c_parts = []
    for d in range(2):
        h0 = jnp.zeros((bsz, W_REC), jnp.float32)
        yc_d, h_ctx = rglru_direction(r_x_c, rconv_w[d], rconv_b[d], w_r[d], b_r[d], w_i[d], b_i[d],
                                      lam[d], h0, reverse=(d == 1))
        yl_d, _ = rglru_direction(r_x_l, rconv_w[d], rconv_b[d], w_r[d], b_r[d], w_i[d], b_i[d],
                                  lam[d], h_ctx, reverse=(d == 1))
        yb_l = yb_l + yl_d
        yb_c_parts.append(yc_d)

    heads = lambda t: t.reshape(bsz, -1, ATT_HEADS, ATT_HEAD_DIM)
    kc_h, vc_h = heads(k_c), heads(v_c)
    yc_l = neighbourhood_attention(heads(q_l), heads(k_l), heads(v_l), kc_h, vc_h, rpb)

    mix_l = jnp.concatenate([ya_l * jax.nn.silu(a_gate_l), yb_l * jax.nn.silu(r_gate_l),
                             yc_l * jax.nn.silu(c_gate_l)], axis=-1)
    xl_new = xl + g_l * (mix_l @ w_out)
    if not need_ctx:
        return xl_new, xc

    ya_c = conformer_conv(a_val_c, a_glu_c, conv_w, conv_b, ln_g, ln_b, w_pw, b_pw)
    yb_c = yb_c_parts[0] + yb_c_parts[1]
    yc_c = context_attention(heads(q_c), kc_h, vc_h)
    mix_c = jnp.concatenate([ya_c * jax.nn.silu(a_gate_c), yb_c * jax.nn.silu(r_gate_c),
                             yc_c * jax.nn.silu(c_gate_c)], axis=-1)
    xc_new = xc + g_c * (mix_c @ w_out)
    return xl_new, xc_new


def setup_inputs(seed: int = 0) -> dict:
    key = jax.random.key(seed)
    ks = jax.random.split(key, 26)
    f32 = jnp.float32
    nrm = lambda k, shape, s: jax.random.normal(k, shape, f32) * s
    d = D_MODEL
    u = jax.random.uniform(ks[21], (DEPTH, 2, W_REC), f32, minval=0.9, maxval=0.999)
    a_base = u ** (1.0 / REC_C)
    lam = jnp.log(a_base) - jnp.log1p(-a_base)
    return {
        "x": nrm(ks[0], (BATCH, SEQ, d), 1.0),
        "c": nrm(ks[1], (BATCH, d), 1.0),
        "ctx": nrm(ks[2], (BATCH, CTX_LEN, d), 1.0),
        "c_ctx": nrm(ks[3], (d,), 1.0),
        "w_ada": nrm(ks[4], (DEPTH, d, 3 * d), 0.3 * d ** -0.5),
        "b_ada": nrm(ks[5], (DEPTH, 3 * d), 0.01),
        "norm_g": 1.0 + nrm(ks[6], (DEPTH, d), 0.02),
        "w_in": nrm(ks[7], (DEPTH, d, N_IN), d ** -0.5),
        "conv_w": nrm(ks[8], (DEPTH, CONV_K, W_CONV), CONV_K ** -0.5),
        "conv_b": nrm(ks[9], (DEPTH, W_CONV), 0.01),
        "ln_g": 1.0 + nrm(ks[10], (DEPTH, W_CONV), 0.02),
        "ln_b": nrm(ks[11], (DEPTH, W_CONV), 0.01),
        "w_pw": nrm(ks[12], (DEPTH, W_CONV, W_CONV), W_CONV ** -0.5),
        "b_pw": nrm(ks[13], (DEPTH, W_CONV), 0.01),
        "rconv_w": nrm(ks[14], (DEPTH, 2, REC_CONV_K, W_REC), REC_CONV_K ** -0.5),
        "rconv_b": nrm(ks[15], (DEPTH, 2, W_REC), 0.01),
        "w_r": nrm(ks[16], (DEPTH, 2, REC_HEADS, REC_BLOCK, REC_BLOCK), REC_BLOCK ** -0.5),
        "b_r": nrm(ks[17], (DEPTH, 2, W_REC), 0.01),
        "w_i": nrm(ks[18], (DEPTH, 2, REC_HEADS, REC_BLOCK, REC_BLOCK), REC_BLOCK ** -0.5),
        "b_i": nrm(ks[19], (DEPTH, 2, W_REC), 0.01),
        "lam": lam,
        "rpb": nrm(ks[20], (DEPTH, ATT_HEADS, 2 * WIN_H - 1, 2 * WIN_W - 1), 0.02),
        "w_out": nrm(ks[22], (DEPTH, D_MIX, d), D_MIX ** -0.5),
        "final_g": 1.0 + nrm(ks[23], (d,), 0.02),
    }


def reference(x, c, ctx, c_ctx, w_ada, b_ada, norm_g, w_in, conv_w, conv_b, ln_g, ln_b, w_pw, b_pw,
              rconv_w, rconv_b, w_r, b_r, w_i, b_i, lam, rpb, w_out, final_g):
    xl, xc = x, ctx
    for l in range(DEPTH):
        xl, xc = hybrid_layer(xl, xc, c, c_ctx, w_ada[l], b_ada[l], norm_g[l], w_in[l], conv_w[l],
                              conv_b[l], ln_g[l], ln_b[l], w_pw[l], b_pw[l], rconv_w[l], rconv_b[l],
                              w_r[l], b_r[l], w_i[l], b_i[l], lam[l], rpb[l], w_out[l],
                              need_ctx=(l < DEPTH - 1))
    return rmsnorm(xl, final_g)
```

```python
import numpy as np
from contextlib import ExitStack
import concourse.bass as bass
import concourse.mybir as mybir
from concourse.bass_utils import run_bass_kernel_spmd

F32 = mybir.dt.float32
BF16 = mybir.dt.bfloat16
AF = mybir.ActivationFunctionType
ALU = mybir.AluOpType
AX = mybir.AxisListType

NC = 8
D = 4096
KC = 32
NCTX = 256
NLAT = 2048
NIN = 13312
DEPTH = 4
NEG = -30000.0
EPS = 1e-6
NTK = 2816
OWN0 = 512
NMIX = NCTX + NLAT
TT = [(0, 256), (256, 256), (512, 512), (1024, 512), (1536, 512), (2048, 512), (2560, 256)]
GROUPS = [[0, 1, 2], [3, 4], [5, 6]]
MT = [0, 2, 3, 4, 5]
C_AVAL, C_AGLU, C_AGATE, C_RX, C_RGATE, C_Q, C_K, C_V, C_CGATE = 0, 8, 16, 24, 32, 40, 56, 72, 88
PP_NG, PP_CW, PP_CB, PP_LG, PP_LB, PP_BPW, PP_RW, PP_RB, PP_BR, PP_BI, PP_LAM = 0, 32, 280, 288, 296, 304, 312, 376, 392, 408, 424
NPP = 440
ADA_SH = 1536


class Buf:
    __slots__ = ("w", "r")

    def __init__(self):
        self.w = None
        self.r = {}


class Tracker:
    def __init__(self, nc, es):
        self.nc = nc
        self.eng = {"pe": nc.tensor, "act": nc.scalar, "dve": nc.vector, "pool": nc.gpsimd, "sp": nc.sync}
        self.sem = {k: es.enter_context(nc.semaphore("s_" + k)) for k in ("pe", "act", "dve", "pool")}
        self.cnt = {k: 0 for k in self.sem}
        self.dq = {}
        for q, n in (("sp", 12), ("pool", 8)):
            self.dq[q] = {"sems": [es.enter_context(nc.semaphore(f"d_{q}{i}")) for i in range(n)],
                          "cnt": [0] * n, "nxt": 0}
        self.csem = es.enter_context(nc.semaphore("s_cc"))
        self.ccnt = 0
        self.waited = {k: {} for k in self.eng}
        self.bufs = {}

    def buf(self, key):
        b = self.bufs.get(key)
        if b is None:
            b = self.bufs[key] = Buf()
        return b

    def _wait(self, eng, ev):
        sem, val = ev
        w = self.waited[eng]
        k = id(sem)
        if w.get(k, 0) >= val:
            return
        self.eng[eng].wait_ge(sem, val)
        w[k] = val

    def _deps(self, eng, outs, ins):
        for b in ins:
            if b.w is not None:
                self._wait(eng, b.w)
        for b in outs:
            if b.w is not None:
                self._wait(eng, b.w)
            for ev in b.r.values():
                self._wait(eng, ev)

    def _post(self, ev, outs, ins):
        k = id(ev[0])
        for b in outs:
            b.w = ev
            b.r = {}
        for b in ins:
            b.r[k] = ev

    def op(self, eng, fn, outs=(), ins=()):
        self._deps(eng, outs, ins)
        ins_ = fn(self.eng[eng])
        self.cnt[eng] += 1
        ins_.then_inc(self.sem[eng], 1)
        ev = (self.sem[eng], self.cnt[eng])
        self._post(ev, outs, ins)
        return ev

    def mm(self, fns, outs, ins):
        self._deps("pe", outs, ins)
        last = None
        for fn in fns:
            last = fn(self.nc.tensor)
        self.cnt["pe"] += 1
        last.then_inc(self.sem["pe"], 1)
        ev = (self.sem["pe"], self.cnt["pe"])
        self._post(ev, outs, ins)
        return ev

    def dma(self, out, in_, outs=(), ins=(), q="sp"):
        dq = self.dq[q]
        i = dq["nxt"] % len(dq["sems"])
        dq["nxt"] += 1
        sem = dq["sems"][i]
        if dq["cnt"][i]:
            self._wait(q, (sem, dq["cnt"][i]))
        self._deps(q, outs, ins)
        self.eng[q].dma_start(out=out, in_=in_).then_inc(sem, 16)
        dq["cnt"][i] += 16
        ev = (sem, dq["cnt"][i])
        self._post(ev, outs, ins)
        return ev

    def coll(self, in_ap, out_ap, outs, ins):
        self._deps("pool", outs, ins)
        self.nc.gpsimd.collective_compute("AllGather", ALU.bypass, replica_groups=[list(range(NC))],
                                          ins=[in_ap], outs=[out_ap]).then_inc(self.csem)
        self.ccnt += 1
        ev = (self.csem, self.ccnt)
        self._post(ev, outs, ins)
        return ev

    def barrier(self):
        evs = [(self.sem[k], self.cnt[k]) for k in self.sem if self.cnt[k]]
        for dq in self.dq.values():
            evs += [(s, c) for s, c in zip(dq["sems"], dq["cnt"]) if c]
        if self.ccnt:
            evs.append((self.csem, self.ccnt))
        for e in self.eng:
            for ev in evs:
                self._wait(e, ev)


def _mk(nc, es):
    T = Tracker(nc, es)
    uid = [0]

    def sb(s, name, shape, dtype=F32):
        uid[0] += 1
        return s.enter_context(nc.sbuf_tensor(f"sb{uid[0]}_{name}", shape, dtype))

    def ps(s, name, shape, dtype=F32):
        uid[0] += 1
        return s.enter_context(nc.psum_tensor(f"ps{uid[0]}_{name}", shape, dtype))

    return T, sb, ps


def build_mod():
    nc = bass.Bass("TRN2", target_bir_lowering=False)
    dt = lambda name, shape, dtype=F32: nc.dram_tensor(name, shape, dtype, kind="ExternalInput").ap()
    cvec = dt("cvec", [128, 64])
    w_ada = dt("w_ada_sh", [DEPTH * D, ADA_SH])
    b_ada = dt("b_ada_sh", [2, DEPTH * ADA_SH])
    out = nc.dram_tensor("modrow", [2, DEPTH * ADA_SH], F32, kind="ExternalOutput").ap()
    with ExitStack() as es:
        T, sb, ps = _mk(nc, es)
        B = T.buf
        cs = sb(es, "cs", [128, 64])
        cs2 = sb(es, "cs2", [128, KC, 2])
        stg = [sb(es, f"astg{i}", [128, 8, 512]) for i in range(3)]
        modrow = sb(es, "modrow", [2, DEPTH * ADA_SH])
        brow = sb(es, "brow", [2, DEPTH * ADA_SH])
        pm = [ps(es, f"pm{i}", [128, 512]) for i in range(2)]
        b_cs, b_mr = B("cs"), B("modrow")
        T.dma(cs[:], cvec, outs=[b_cs])
        T.dma(brow[:], b_ada, outs=[b_mr])
        T.op("act", lambda e: e.activation(out=cs2[:, :, 0], in_=cs[:, 0:32], func=AF.Silu), outs=[b_cs], ins=[b_cs])
        T.op("act", lambda e: e.activation(out=cs2[:, :, 1], in_=cs[:, 32:64], func=AF.Silu), outs=[b_cs], ins=[b_cs])
        k = 0
        for l in range(DEPTH):
            for ct in range(3):
                pmt = pm[(l * 3 + ct) % 2]
                bpm = B(("pm", (l * 3 + ct) % 2))
                for qd in range(4):
                    st = stg[k % 3]
                    bst = B(("astg", k % 3))
                    k += 1
                    r0 = l * D + qd * 1024
                    T.dma(st[:], w_ada[r0:r0 + 1024, ct * 512:(ct + 1) * 512].rearrange("(k p) c -> p k c", p=128), outs=[bst])
                    T.mm([lambda e, st=st, kk=kk, qd=qd, pmt=pmt: e.matmul(pmt[0:2, :], lhsT=cs2[:, qd * 8 + kk, :], rhs=st[:, kk, :],
                                                                         start=(qd == 0 and kk == 0), stop=(qd == 3 and kk == 7))
                          for kk in range(8)], outs=[bpm], ins=[bst, b_cs])
                c0 = l * ADA_SH + ct * 512
                T.op("dve", lambda e, pmt=pmt, c0=c0: e.tensor_tensor(out=modrow[:, c0:c0 + 512], in0=pmt[0:2, :], in1=brow[:, c0:c0 + 512], op=ALU.add),
                     outs=[b_mr], ins=[bpm, b_mr])
        T.dma(out, modrow[:], outs=[B("out")], ins=[b_mr], q="pool")
        T.barrier()
    return nc


def build_layer(mode):
    last = (mode == "L")
    nc = bass.Bass("TRN2", target_bir_lowering=False)
    dt = lambda name, shape, dtype=F32: nc.dram_tensor(name, shape, dtype, kind="ExternalInput").ap()
    sc = lambda name, shape, dtype=F32: nc.dram_tensor(name, shape, dtype).ap()
    xT = dt("xT", [KC * 128, NTK])
    ppd = dt("pp", [128, NPP])
    modt = dt("modt", [128, 192])
    w_ri = dt("w_ri", [32 * 128, 128])
    flagd = dt("flags", [128, 18])
    if mode == "A":
        w_in = dt("w_rx", [D, 1024])
        car_out = nc.dram_tensor("car", [128, 32], F32, kind="ExternalOutput").ap()
    else:
        w_in = dt("w_in", [D, NIN])
        w_out = dt("w_out", [D, D])
        w_pw = dt("w_pw", [1024, 1024])
        tfd = dt("tf", [16 * 128, 7 * 128])
        msd = dt("ms", [128, 5 * 7 * 128])
        gcard = dt("gcar", [128, NC * 32])
        identd = dt("ident", [128, 128])
        if last:
            fg = dt("final_g_t", [128, KC])
            outT = nc.dram_tensor("outT", [KC * 128, NLAT], F32, kind="ExternalOutput").ap()
        else:
            xo = nc.dram_tensor("xo", [KC * 128, NMIX], F32, kind="ExternalOutput").ap()
    RX = sc("RX", [8 * 128, NTK])
    SA = sc("SA", [16 * 128, NLAT])
    SG = sc("SG", [16 * 128, NLAT])
    if mode != "A":
        XT = sc("XT", [KC * 128, NMIX])
        ZV = sc("ZV", [8 * 128, NTK], BF16)
        ZS = sc("ZS", [8 * 128, NTK], BF16)
        GA = sc("GA", [8 * 128, NTK], BF16)
        GR = sc("GR", [8 * 128, NTK], BF16)
        QT = sc("QT", [16 * 128, NTK], BF16)
        KT = sc("KT", [16 * 128, NTK], BF16)
        VT = sc("VT", [16 * 128, NTK], BF16)
        GC = sc("GC", [16 * 128, NTK], BF16)
        MIX = sc("MIX", [KC * 128, NMIX], BF16)

    with ExitStack() as es:
        T, sb, ps = _mk(nc, es)
        B = T.buf
        ones = sb(es, "ones", [128, 128])
        modT = sb(es, "modT", [128, 192])
        gmod = sb(es, "gmod", [128, 64])
        pp = sb(es, "pp", [128, NPP])
        clam = sb(es, "clam", [128, 32])
        flags = sb(es, "flags", [128, 18])
        b_const = B("const")
        T.dma(pp[:], ppd, outs=[b_const])
        T.dma(modT[:], modt, outs=[b_const])
        T.dma(flags[:], flagd, outs=[b_const])
        T.op("dve", lambda e: e.memset(ones[:], 1.0), outs=[b_const])
        if mode != "A":
            ident = sb(es, "ident", [128, 128])
            identb = sb(es, "identb", [128, 128], BF16)
            onesb = sb(es, "onesb", [128, 128], BF16)
            ms = sb(es, "ms", [128, 5 * 896])
            T.dma(ident[:], identd, outs=[b_const])
            T.dma(ms[:], msd, outs=[b_const])
            T.op("dve", lambda e: e.memset(onesb[:], 1.0), outs=[b_const])
            T.op("dve", lambda e: e.tensor_copy(out=identb[:], in_=ident[:]), outs=[b_const], ins=[b_const])
            if last:
                fgt = sb(es, "fgt", [128, KC])
                T.dma(fgt[:], fg, outs=[b_const])
        pl = lambda off, n=1: pp[:, off:off + n]
        cl_, cl2_ = clam[:, 0:16], clam[:, 16:32]
        T.op("act", lambda e: e.activation(out=cl_, in_=pl(PP_LAM, 16), func=AF.Exp, scale=-1.0), outs=[b_const], ins=[b_const])
        T.op("dve", lambda e: e.tensor_scalar(out=cl_, in0=cl_, scalar1=1.0, scalar2=None, op0=ALU.add), outs=[b_const], ins=[b_const])
        T.op("act", lambda e: e.activation(out=cl_, in_=cl_, func=AF.Ln), outs=[b_const], ins=[b_const])
        T.op("dve", lambda e: e.tensor_scalar(out=cl2_, in0=cl_, scalar1=-16.0, scalar2=None, op0=ALU.mult), outs=[b_const], ins=[b_const])
        T.op("dve", lambda e: e.tensor_scalar(out=cl_, in0=cl_, scalar1=-8.0, scalar2=None, op0=ALU.mult), outs=[b_const], ins=[b_const])
        for j in range(2):
            T.op("dve", lambda e, j=j: e.scalar_tensor_tensor(out=gmod[:, j * 32:(j + 1) * 32], in0=modT[:, j * 96 + 32:j * 96 + 64], scalar=1.0,
                                                              in1=pl(PP_NG, 32), op0=ALU.add, op1=ALU.mult), outs=[b_const], ins=[b_const])
        T.barrier()

        def gemm(s, Wd, nK, coltiles, rhs, b_rhs, tiles, evac, tag, skip=None):
            nq = nK // 8
            wb = [sb(s, f"{tag}wb{i}", [128, nK, 256], BF16) for i in range(2)]
            stg = [sb(s, f"{tag}stg{i}", [128, 8, 256]) for i in range(3)]
            pst = [ps(s, f"{tag}ps{i}", [128, 512]) for i in range(4)]
            st = {"k": 0, "p": 0}

            def load(ti):
                c0 = coltiles[ti]
                for qd in range(nq):
                    i = st["k"] % 3
                    st["k"] += 1
                    bst = B((tag, "stg", i))
                    r0 = qd * 1024
                    T.dma(stg[i][:], Wd[r0:r0 + 1024, c0:c0 + 256].rearrange("(k p) c -> p k c", p=128), outs=[bst])
                    ce = ("act", "dve", "act", "pool")[qd % 4]
                    if ce == "act":
                        T.op("act", lambda e, i=i, qd=qd, ti=ti: e.activation(out=wb[ti % 2][:, qd * 8:(qd + 1) * 8, :], in_=stg[i][:], func=AF.Copy),
                             outs=[B((tag, "wb", ti % 2, qd))], ins=[bst])
                    else:
                        T.op(ce, lambda e, i=i, qd=qd, ti=ti: e.tensor_copy(out=wb[ti % 2][:, qd * 8:(qd + 1) * 8, :], in_=stg[i][:]),
                             outs=[B((tag, "wb", ti % 2, qd))], ins=[bst])

            load(0)
            for ti in range(len(coltiles)):
                if ti + 1 < len(coltiles):
                    load(ti + 1)
                wbufs = [B((tag, "wb", ti % 2, qd)) for qd in range(nq)]
                for ch in range(2):
                    for (tidx, t0, nt) in tiles:
                        if skip is not None and skip(coltiles[ti] // 128 + ch, tidx):
                            continue
                        pi = st["p"] % 4
                        st["p"] += 1
                        bps = B((tag, "ps", pi))
                        T.mm([lambda e, kk=kk, pi=pi, ti=ti, ch=ch, t0=t0, nt=nt: e.matmul(
                            pst[pi][:, 0:nt], lhsT=wb[ti % 2][:, kk, ch * 128:(ch + 1) * 128], rhs=rhs[:, kk, t0:t0 + nt],
                            start=(kk == 0), stop=(kk == nK - 1)) for kk in range(nK)], outs=[bps], ins=wbufs + [b_rhs])
                        evac(coltiles[ti] // 128 + ch, tidx, nt, pst[pi], bps)

        for gi, grp in enumerate(GROUPS):
            g0 = TT[grp[0]][0]
            gn = sum(TT[t][1] for t in grp)
            with ExitStack() as s1:
                hT = sb(s1, "hT", [128, KC, gn], BF16)
                b_hT = B(("hT", gi))
                with ExitStack() as s2:
                    xt = [sb(s2, f"nx{i}", [128, 512]) for i in range(3)]
                    sq = [sb(s2, f"nsq{i}", [128, 512]) for i in range(2)]
                    rstd = sb(s2, "rstd", [128, 512])
                    pss = [ps(s2, f"nps{i}", [128, 512]) for i in range(2)]
                    kx = 0
                    for tix, t in enumerate(grp):
                        t0, nt = TT[t]
                        jj = 1 if t == 0 else 0
                        bpss = B(("nps", tix % 2))
                        pst_ = pss[tix % 2]
                        for kc in range(KC):
                            xi = kx % 3
                            kx += 1
                            bxt = B(("nx", xi))
                            T.dma(xt[xi][:, 0:nt], xT[kc * 128:(kc + 1) * 128, t0:t0 + nt], outs=[bxt])
                            bsq = B(("nsq", kc % 2))
                            T.op("act", lambda e, xi=xi, kc=kc, nt=nt: e.activation(out=sq[kc % 2][:, 0:nt], in_=xt[xi][:, 0:nt], func=AF.Square),
                                 outs=[bsq], ins=[bxt])
                            T.mm([lambda e, kc=kc, nt=nt, pst_=pst_: e.matmul(pst_[:, 0:nt], lhsT=ones[:], rhs=sq[kc % 2][:, 0:nt],
                                                                              start=(kc == 0), stop=(kc == KC - 1))], outs=[bpss], ins=[bsq, b_const])
                        b_rstd = B("rstd")
                        T.op("dve", lambda e, nt=nt, pst_=pst_: e.tensor_scalar(out=rstd[:, 0:nt], in0=pst_[:, 0:nt], scalar1=1.0 / D, scalar2=EPS,
                                                                               op0=ALU.mult, op1=ALU.add), outs=[b_rstd], ins=[bpss])
                        T.op("act", lambda e, nt=nt: e.activation(out=rstd[:, 0:nt], in_=rstd[:, 0:nt], func=AF.Sqrt), outs=[b_rstd], ins=[b_rstd])
                        T.op("dve", lambda e, nt=nt: e.reciprocal(out=rstd[:, 0:nt], in_=rstd[:, 0:nt]), outs=[b_rstd], ins=[b_rstd])
                        for kc in range(KC):
                            xi = kx % 3
                            kx += 1
                            bxt = B(("nx", xi))
                            T.dma(xt[xi][:, 0:nt], xT[kc * 128:(kc + 1) * 128, t0:t0 + nt], outs=[bxt])
                            T.op("dve", lambda e, xi=xi, nt=nt: e.tensor_tensor(out=xt[xi][:, 0:nt], in0=xt[xi][:, 0:nt], in1=rstd[:, 0:nt], op=ALU.mult),
                                 outs=[bxt], ins=[bxt, b_rstd])
                            T.op("act", lambda e, xi=xi, nt=nt, kc=kc, jj=jj, t0=t0: e.activation(
                                out=hT[:, kc, t0 - g0:t0 - g0 + nt], in_=xt[xi][:, 0:nt], func=AF.Identity,
                                scale=gmod[:, jj * 32 + kc:jj * 32 + kc + 1], bias=modT[:, jj * 96 + kc:jj * 96 + kc + 1]),
                                 outs=[b_hT], ins=[bxt, b_const])
                    T.barrier()
                with ExitStack() as s2:
                    evb = [sb(s2, f"evb{i}", [128, 512], BF16) for i in range(4)]
                    evf = [sb(s2, f"evf{i}", [128, 512]) for i in range(2)]
                    ek = {"b": 0, "f": 0, "e": 0}

                    def evac_in(chunk, tidx, nt, pt, bps):
                        t0 = TT[tidx][0]
                        if mode == "A":
                            chunk = chunk + C_RX
                        if C_RX <= chunk < C_RGATE:
                            i = ek["f"] % 2
                            ek["f"] += 1
                            bo = B(("evf", i))
                            T.op("dve", lambda e: e.tensor_copy(out=evf[i][:, 0:nt], in_=pt[:, 0:nt]), outs=[bo], ins=[bps])
                            c = chunk - C_RX
                            T.dma(RX[c * 128:(c + 1) * 128, t0:t0 + nt], evf[i][:, 0:nt], outs=[B(("RX", c))], ins=[bo], q="pool")
                            return
                        i = ek["b"] % 4
                        ek["b"] += 1
                        bo = B(("evb", i))
                        o = evb[i]
                        if chunk < C_AGLU:
                            dst, name, c, fn = ZV, "ZV", chunk - C_AVAL, None
                        elif chunk < C_AGATE:
                            dst, name, c, fn = ZS, "ZS", chunk - C_AGLU, AF.Sigmoid
                        elif chunk < C_RX:
                            dst, name, c, fn = GA, "GA", chunk - C_AGATE, AF.Silu
                        elif chunk < C_Q:
                            dst, name, c, fn = GR, "GR", chunk - C_RGATE, AF.Silu
                        elif chunk < C_K:
                            dst, name, c, fn = QT, "QT", chunk - C_Q, None
                        elif chunk < C_V:
                            dst, name, c, fn = KT, "KT", chunk - C_K, None
                        elif chunk < C_CGATE:
                            dst, name, c, fn = VT, "VT", chunk - C_V, None
                        else:
                            dst, name, c, fn = GC, "GC", chunk - C_CGATE, AF.Silu
                        if fn is not None:
                            T.op("act", lambda e: e.activation(out=o[:, 0:nt], in_=pt[:, 0:nt], func=fn), outs=[bo], ins=[bps])
                        else:
                            ek["e"] += 1
                            if ek["e"] % 2:
                                T.op("dve", lambda e: e.tensor_copy(out=o[:, 0:nt], in_=pt[:, 0:nt]), outs=[bo], ins=[bps])
                            else:
                                T.op("act", lambda e: e.activation(out=o[:, 0:nt], in_=pt[:, 0:nt], func=AF.Copy), outs=[bo], ins=[bps])
                        T.dma(dst[c * 128:(c + 1) * 128, t0:t0 + nt], o[:, 0:nt], outs=[B((name, c))], ins=[bo], q="pool")

                    tiles = [(t, TT[t][0] - g0, TT[t][1]) for t in grp]
                    ncols = 1024 if mode == "A" else NIN
                    need_halo = lambda ch: ch < C_AGATE or C_RX <= ch < C_RGATE or C_K <= ch < C_CGATE
                    skip_in = None if mode == "A" else (lambda ch, tidx: tidx in (1, 6) and not need_halo(ch))
                    gemm(s2, w_in, KC, [c * 256 for c in range(ncols // 256)], hT, b_hT, tiles, evac_in, "gi", skip=skip_in)
                    T.barrier()

        if mode != "A":
            with ExitStack() as s2:
                wst = sb(s2, "cwst", [128, 8, 256])
                wpb = sb(s2, "wpb", [128, 8, 1024], BF16)
                b_wpb = B("wpb")
                for ct in range(4):
                    T.dma(wst[:], w_pw[:, ct * 256:(ct + 1) * 256].rearrange("(k p) c -> p k c", p=128), outs=[B("cwst")])
                    T.op("pool", lambda e, ct=ct: e.tensor_copy(out=wpb[:, :, ct * 256:(ct + 1) * 256], in_=wst[:]), outs=[b_wpb], ins=[B("cwst")])
                vb = [sb(s2, f"cvb{i}", [128, 542], BF16) for i in range(2)]
                sgb = [sb(s2, f"csb{i}", [128, 542], BF16) for i in range(2)]
                ub = [sb(s2, f"cub{i}", [128, 542]) for i in range(2)]
                cvo = sb(s2, "cvo", [128, 8, 512])
                sqc = [sb(s2, f"csq{i}", [128, 512]) for i in range(2)]
                mean = sb(s2, "cmean", [128, 512])
                rs = sb(s2, "crs", [128, 512])
                u2 = sb(s2, "cu2", [128, 8, 512], BF16)
                gt = [sb(s2, f"cgt{i}", [128, 512], BF16) for i in range(2)]
                mo = [sb(s2, f"cmo{i}", [128, 512], BF16) for i in range(2)]
                p1 = ps(s2, "cp1", [128, 512])
                p2 = ps(s2, "cp2", [128, 512])
                pg = [ps(s2, f"cpg{i}", [128, 512]) for i in range(2)]
                for tidx in MT:
                    t0, nt = TT[tidx]
                    m0 = t0 if tidx == 0 else t0 - 256
                    b_cvo = [B(("cvo", c)) for c in range(8)]
                    for c in range(8):
                        i = c % 2
                        bub, bvb = B(("cub", i)), B(("cvb", i))
                        isctx = (tidx == 0)
                        a0 = 15 if isctx else 0
                        a1 = (15 + nt) if isctx else (30 + nt)
                        s0 = t0 - 15 + a0
                        T.dma(vb[i][:, a0:a1], ZV[c * 128:(c + 1) * 128, s0:s0 + a1 - a0], outs=[bvb], ins=[B(("ZV", c))])
                        T.dma(sgb[i][:, a0:a1], ZS[c * 128:(c + 1) * 128, s0:s0 + a1 - a0], outs=[bvb], ins=[B(("ZS", c))])
                        T.op("dve", lambda e, i=i, a0=a0, a1=a1: e.tensor_tensor(out=ub[i][:, a0:a1], in0=vb[i][:, a0:a1], in1=sgb[i][:, a0:a1], op=ALU.mult),
                             outs=[bub], ins=[bvb])
                        if isctx:
                            T.op("dve", lambda e, i=i: e.memset(ub[i][:, 0:15], 0.0), outs=[bub])
                            T.op("dve", lambda e, i=i, nt=nt: e.memset(ub[i][:, 15 + nt:30 + nt], 0.0), outs=[bub])
                        if tidx == 2:
                            T.op("dve", lambda e, i=i: e.tensor_scalar(out=ub[i][:, 0:15], in0=ub[i][:, 0:15], scalar1=flags[:, 0:1], scalar2=None, op0=ALU.mult),
                                 outs=[bub], ins=[bub, b_const])
                        if tidx == 5:
                            T.op("dve", lambda e, i=i, nt=nt: e.tensor_scalar(out=ub[i][:, 15 + nt:30 + nt], in0=ub[i][:, 15 + nt:30 + nt], scalar1=flags[:, 1:2], scalar2=None,
                                                                           op0=ALU.mult), outs=[bub], ins=[bub, b_const])
                        cw = lambda j, c=c: pl(PP_CW + c * 31 + j)
                        T.op("dve", lambda e, i=i, c=c, nt=nt: e.tensor_scalar(out=cvo[:, c, 0:nt], in0=ub[i][:, 0:nt], scalar1=cw(0), scalar2=pl(PP_CB + c),
                                                                             op0=ALU.mult, op1=ALU.add), outs=[b_cvo[c]], ins=[bub, b_const])
                        for j in range(1, 31):
                            T.op("dve", lambda e, i=i, c=c, nt=nt, j=j: e.scalar_tensor_tensor(out=cvo[:, c, 0:nt], in0=ub[i][:, j:j + nt], scalar=cw(j),
                                                                                             in1=cvo[:, c, 0:nt], op0=ALU.mult, op1=ALU.add),
                                 outs=[b_cvo[c]], ins=[bub, b_const, b_cvo[c]])
                    bp1, bp2 = B("cp1"), B("cp2")
                    for c in range(8):
                        T.mm([lambda e, c=c, nt=nt: e.matmul(p1[:, 0:nt], lhsT=ones[:], rhs=cvo[:, c, 0:nt], start=(c == 0), stop=(c == 7))],
                             outs=[bp1], ins=[b_cvo[c], b_const])
                    for c in range(8):
                        bsq = B(("csq", c % 2))
                        T.op("act", lambda e, c=c, nt=nt: e.activation(out=sqc[c % 2][:, 0:nt], in_=cvo[:, c, 0:nt], func=AF.Square), outs=[bsq], ins=[b_cvo[c]])
                        T.mm([lambda e, c=c, nt=nt: e.matmul(p2[:, 0:nt], lhsT=ones[:], rhs=sqc[c % 2][:, 0:nt], start=(c == 0), stop=(c == 7))],
                             outs=[bp2], ins=[bsq, b_const])
                    b_st = B("cstat")
                    T.op("dve", lambda e, nt=nt: e.tensor_scalar(out=mean[:, 0:nt], in0=p1[:, 0:nt], scalar1=1.0 / 1024, scalar2=None, op0=ALU.mult), outs=[b_st], ins=[bp1])
                    T.op("dve", lambda e, nt=nt: e.tensor_tensor(out=rs[:, 0:nt], in0=mean[:, 0:nt], in1=mean[:, 0:nt], op=ALU.mult), outs=[b_st], ins=[b_st])
                    T.op("dve", lambda e, nt=nt: e.scalar_tensor_tensor(out=rs[:, 0:nt], in0=p2[:, 0:nt], scalar=1.0 / 1024, in1=rs[:, 0:nt], op0=ALU.mult, op1=ALU.subtract),
                         outs=[b_st], ins=[b_st, bp2])
                    T.op("dve", lambda e, nt=nt: e.tensor_scalar(out=rs[:, 0:nt], in0=rs[:, 0:nt], scalar1=EPS, scalar2=None, op0=ALU.add), outs=[b_st], ins=[b_st])
                    T.op("act", lambda e, nt=nt: e.activation(out=rs[:, 0:nt], in_=rs[:, 0:nt], func=AF.Sqrt), outs=[b_st], ins=[b_st])
                    T.op("dve", lambda e, nt=nt: e.reciprocal(out=rs[:, 0:nt], in_=rs[:, 0:nt]), outs=[b_st], ins=[b_st])
                    b_u2 = B("cu2")
                    for c in range(8):
                        T.op("dve", lambda e, c=c, nt=nt: e.tensor_tensor(out=cvo[:, c, 0:nt], in0=cvo[:, c, 0:nt], in1=mean[:, 0:nt], op=ALU.subtract),
                             outs=[b_cvo[c]], ins=[b_cvo[c], b_st])
                        T.op("dve", lambda e, c=c, nt=nt: e.tensor_tensor(out=cvo[:, c, 0:nt], in0=cvo[:, c, 0:nt], in1=rs[:, 0:nt], op=ALU.mult),
                             outs=[b_cvo[c]], ins=[b_cvo[c], b_st])
                        T.op("act", lambda e, c=c, nt=nt: e.activation(out=u2[:, c, 0:nt], in_=cvo[:, c, 0:nt], func=AF.Silu, scale=pl(PP_LG + c), bias=pl(PP_LB + c)),
                             outs=[b_u2], ins=[b_cvo[c], b_const])
                    for co in range(8):
                        i = co % 2
                        bpg, bgt, bmo = B(("cpg", i)), B(("cgt", i)), B(("cmo", i))
                        T.dma(gt[i][:, 0:nt], GA[co * 128:(co + 1) * 128, t0:t0 + nt], outs=[bgt], ins=[B(("GA", co))])
                        T.mm([lambda e, ci=ci, co=co, i=i, nt=nt: e.matmul(pg[i][:, 0:nt], lhsT=wpb[:, ci, co * 128:(co + 1) * 128], rhs=u2[:, ci, 0:nt],
                                                                          start=(ci == 0), stop=(ci == 7)) for ci in range(8)], outs=[bpg], ins=[b_u2, b_wpb])
                        T.op("dve", lambda e, i=i, co=co, nt=nt: e.scalar_tensor_tensor(out=mo[i][:, 0:nt], in0=pg[i][:, 0:nt], scalar=pl(PP_BPW + co), in1=gt[i][:, 0:nt],
                                                                                     op0=ALU.add, op1=ALU.mult), outs=[bmo], ins=[bpg, bgt, b_const])
                        T.dma(MIX[co * 128:(co + 1) * 128, m0:m0 + nt], mo[i][:, 0:nt], outs=[B(("MIX", co))], ins=[bmo], q="pool")
                T.barrier()

        with ExitStack() as s2:
            wst = sb(s2, "rwst", [128, 32, 128])
            wrb = sb(s2, "wrb", [128, 32, 128], BF16)
            b_wrb = B("wrb")
            T.dma(wst[:], w_ri.rearrange("(m p) c -> p m c", p=128), outs=[B("rwst")])
            T.op("pool", lambda e: e.tensor_copy(out=wrb[:], in_=wst[:]), outs=[b_wrb], ins=[B("rwst")])
            pr = [ps(s2, f"rp{i}", [128, 512]) for i in range(4)]
            ybc = sb(s2, "rybc", [128, 8, NCTX])
            car = sb(s2, "rcar", [128, 32])
            ectx = sb(s2, "rectx", [128, 16])
            rsum = sb(s2, "rsum", [128, 1])
            gtc = sb(s2, "rgtc", [128, 8, NCTX], BF16)
            moc = sb(s2, "rmoc", [128, 8, NCTX], BF16)
            gc_ = sb(s2, "rgc", [128, 8, 32])
            hin = sb(s2, "rhin", [128, 16])
            tmp = sb(s2, "rtmp", [128, 8])
            s3 = ExitStack()
            xb = sb(s3, "rxb", [128, 6 + NCTX + 6 + NLAT])
            LB = NCTX + 6
            xc = sb(s3, "rxc", [128, NMIX])
            xcb = sb(s3, "rxcb", [128, NMIX], BF16)
            rr = sb(s3, "rr", [128, NMIX])
            ii = sb(s3, "ri", [128, NMIX])
            aa = sb(s3, "ra", [128, NMIX])
            gg = sb(s3, "rg", [128, NMIX])
            hs = sb(s3, "rhs", [128, NMIX])
            b_car, b_ybc, b_ectx = B("car"), B("ybc"), B("ectx")
            MTT = [(0, 256), (256, 512), (768, 512), (1280, 512), (1792, 512)]
            pk_ = 0
            for hd in range(8):
                b_xb = B("rxb")
                T.op("dve", lambda e: e.memset(xb[:, 0:NCTX + 6], 0.0), outs=[b_xb])
                T.dma(xb[:, 3:3 + NCTX], RX[hd * 128:(hd + 1) * 128, 0:NCTX], outs=[b_xb], ins=[B(("RX", hd))])
                T.dma(xb[:, LB:LB + 6 + NLAT], RX[hd * 128:(hd + 1) * 128, OWN0 - 3:OWN0 + NLAT + 3], outs=[b_xb], ins=[B(("RX", hd))])
                T.op("dve", lambda e: e.tensor_scalar(out=xb[:, LB:LB + 3], in0=xb[:, LB:LB + 3], scalar1=flags[:, 0:1], scalar2=None, op0=ALU.mult),
                     outs=[b_xb], ins=[b_xb, b_const])
                T.op("dve", lambda e: e.tensor_scalar(out=xb[:, LB + 3 + NLAT:LB + 6 + NLAT], in0=xb[:, LB + 3 + NLAT:LB + 6 + NLAT], scalar1=flags[:, 1:2], scalar2=None,
                                                      op0=ALU.mult), outs=[b_xb], ins=[b_xb, b_const])
                for d in range(2):
                    dh = d * 8 + hd
                    off = 0 if d == 0 else 3
                    rw = lambda j: pl(PP_RW + dh * 4 + j)
                    b_xc = B("rxc")
                    for (o0, src0, n) in ((0, off, NCTX), (NCTX, LB + off, NLAT)):
                        T.op("dve", lambda e, o0=o0, src0=src0, n=n: e.tensor_scalar(out=xc[:, o0:o0 + n], in0=xb[:, src0:src0 + n], scalar1=rw(0), scalar2=pl(PP_RB + dh),
                                                                                  op0=ALU.mult, op1=ALU.add), outs=[b_xc], ins=[b_xb, b_const])
                        for j in range(1, 4):
                            T.op("dve", lambda e, o0=o0, src0=src0, n=n, j=j: e.scalar_tensor_tensor(out=xc[:, o0:o0 + n], in0=xb[:, src0 + j:src0 + j + n], scalar=rw(j),
                                                                                                  in1=xc[:, o0:o0 + n], op0=ALU.mult, op1=ALU.add),
                                 outs=[b_xc], ins=[b_xb, b_const, b_xc])
                    b_xcb = B("rxcb")
                    T.op("pool", lambda e: e.tensor_copy(out=xcb[:], in_=xc[:]), outs=[b_xcb], ins=[b_xc])
                    b_rr, b_ii = B("rr"), B("ri")
                    for which, (dst, bd, boff) in enumerate(((rr, b_rr, PP_BR), (ii, b_ii, PP_BI))):
                        m = (d * 2 + which) * 8 + hd
                        for (t0, nt) in MTT:
                            pi = pk_ % 4
                            pk_ += 1
                            bp_ = B(("rp", pi))
                            T.mm([lambda e, pi=pi, m=m, t0=t0, nt=nt: e.matmul(pr[pi][:, 0:nt], lhsT=wrb[:, m, :], rhs=xcb[:, t0:t0 + nt], start=True, stop=True)],
                                 outs=[bp_], ins=[b_xcb, b_wrb])
                            T.op("act", lambda e, pi=pi, dst=dst, t0=t0, nt=nt, boff=boff: e.activation(out=dst[:, t0:t0 + nt], in_=pr[pi][:, 0:nt], func=AF.Sigmoid,
                                                                                                          bias=pl(boff + dh)), outs=[bd], ins=[bp_, b_const])
                    cl = clam[:, dh:dh + 1]
                    cl2 = clam[:, 16 + dh:16 + dh + 1]
                    b_aa, b_gg, b_hs = B("ra"), B("rg"), B("rhs")
                    T.op("act", lambda e: e.activation(out=aa[:], in_=rr[:], func=AF.Exp, scale=cl), outs=[b_aa], ins=[b_rr, b_const])
                    T.op("act", lambda e: e.activation(out=gg[:], in_=rr[:], func=AF.Exp, scale=cl2), outs=[b_gg], ins=[b_rr, b_const])
                    T.op("dve", lambda e: e.tensor_scalar(out=gg[:], in0=gg[:], scalar1=-1.0, scalar2=1.0, op0=ALU.mult, op1=ALU.add), outs=[b_gg], ins=[b_gg])
                    T.op("dve", lambda e: e.tensor_scalar(out=gg[:], in0=gg[:], scalar1=0.0, scalar2=None, op0=ALU.max), outs=[b_gg], ins=[b_gg])
                    T.op("act", lambda e: e.activation(out=gg[:], in_=gg[:], func=AF.Sqrt), outs=[b_gg], ins=[b_gg])
                    T.op("dve", lambda e: e.tensor_tensor(out=gg[:], in0=gg[:], in1=ii[:], op=ALU.mult), outs=[b_gg], ins=[b_gg, b_ii])
                    T.op("dve", lambda e: e.tensor_tensor(out=gg[:], in0=gg[:], in1=xc[:], op=ALU.mult), outs=[b_gg], ins=[b_gg, b_xc])
                    b_rs = B("rsum")
                    T.op("dve", lambda e: e.reduce_sum(out=rsum[:], in_=rr[:, NCTX:NMIX], axis=AX.X), outs=[b_rs], ins=[b_rr])
                    T.op("act", lambda e, dh=dh: e.activation(out=car[:, dh * 2:dh * 2 + 1], in_=rsum[:], func=AF.Exp, scale=cl), outs=[b_car], ins=[b_rs, b_const])
                    if d == 0:
                        T.op("dve", lambda e: e.tensor_tensor_scan(out=hs[:, 0:NCTX], data0=aa[:, 0:NCTX], data1=gg[:, 0:NCTX], initial=0.0, op0=ALU.mult, op1=ALU.add),
                             outs=[b_hs], ins=[b_aa, b_gg])
                        T.op("dve", lambda e: e.tensor_tensor_scan(out=hs[:, NCTX:NMIX], data0=aa[:, NCTX:NMIX], data1=gg[:, NCTX:NMIX], initial=0.0, op0=ALU.mult, op1=ALU.add),
                             outs=[b_hs], ins=[b_aa, b_gg])
                        T.op("dve", lambda e, hd=hd: e.tensor_copy(out=ybc[:, hd, :], in_=hs[:, 0:NCTX]), outs=[b_ybc], ins=[b_hs])
                        e_c, e_l = NCTX - 1, NMIX - 1
                    else:
                        T.op("dve", lambda e: e.tensor_tensor_scan(out=hs[:, 0:NCTX][:, ::-1], data0=aa[:, 0:NCTX][:, ::-1], data1=gg[:, 0:NCTX][:, ::-1], initial=0.0,
                                                                   op0=ALU.mult, op1=ALU.add), outs=[b_hs], ins=[b_aa, b_gg])
                        T.op("dve", lambda e: e.tensor_tensor_scan(out=hs[:, NCTX:NMIX][:, ::-1], data0=aa[:, NCTX:NMIX][:, ::-1], data1=gg[:, NCTX:NMIX][:, ::-1], initial=0.0,
                                                                   op0=ALU.mult, op1=ALU.add), outs=[b_hs], ins=[b_aa, b_gg])
                        T.op("dve", lambda e, hd=hd: e.tensor_tensor(out=ybc[:, hd, :], in0=ybc[:, hd, :], in1=hs[:, 0:NCTX], op=ALU.add), outs=[b_ybc], ins=[b_hs, b_ybc])
                        e_c, e_l = 0, NCTX
                    T.op("dve", lambda e, dh=dh, e_l=e_l: e.tensor_copy(out=car[:, dh * 2 + 1:dh * 2 + 2], in_=hs[:, e_l:e_l + 1]), outs=[b_car], ins=[b_hs])
                    T.op("dve", lambda e, dh=dh, e_c=e_c: e.tensor_copy(out=ectx[:, dh:dh + 1], in_=hs[:, e_c:e_c + 1]), outs=[b_ectx], ins=[b_hs])
                    if mode != "A":
                        T.dma(SA[dh * 128:(dh + 1) * 128, :], aa[:, NCTX:NMIX], outs=[B(("SA", dh))], ins=[b_aa], q="pool")
                        T.dma(SG[dh * 128:(dh + 1) * 128, :], gg[:, NCTX:NMIX], outs=[B(("SG", dh))], ins=[b_gg], q="pool")
            if mode == "A":
                T.dma(car_out, car[:], outs=[B("out")], ins=[b_car], q="pool")
                T.barrier()
                s3.close()
            else:
                if not last:
                    T.dma(gtc[:], GR.rearrange("(c p) t -> p c t", p=128)[:, :, 0:NCTX], outs=[B("rgtc")], ins=[B(("GR", c)) for c in range(8)])
                    T.op("dve", lambda e: e.tensor_tensor(out=moc[:], in0=ybc[:], in1=gtc[:], op=ALU.mult), outs=[B("rmoc")], ins=[b_ybc, B("rgtc")])
                    T.dma(MIX.rearrange("(c p) t -> p c t", p=128)[:, 8:16, 0:NCTX], moc[:], outs=[B(("MIX", 8 + c)) for c in range(8)], ins=[B("rmoc")], q="pool")
                T.dma(gc_[:], gcard.rearrange("p (r c) -> p r c", c=32), outs=[B("rgc")])
                b_hin, b_tmp = B("hin"), B("rtmp")
                T.op("dve", lambda e: e.tensor_copy(out=hin[:], in_=ectx[:]), outs=[b_hin], ins=[b_ectx])
                for d in range(2):
                    order = range(NC) if d == 0 else range(NC - 1, -1, -1)
                    S = hin[:, d * 8:(d + 1) * 8]
                    for r in order:
                        g3 = gc_[:, r, d * 16:(d + 1) * 16].rearrange("p (h two) -> p h two", two=2)
                        A_r, E_r = g3[:, :, 0], g3[:, :, 1]
                        mcol = flags[:, 2 + d * 8 + r:2 + d * 8 + r + 1]
                        T.op("dve", lambda e, A_r=A_r, S=S: e.scalar_tensor_tensor(out=tmp[:], in0=A_r, scalar=-1.0, in1=S, op0=ALU.add, op1=ALU.mult),
                             outs=[b_tmp], ins=[B("rgc"), b_hin])
                        T.op("dve", lambda e, E_r=E_r: e.tensor_tensor(out=tmp[:], in0=tmp[:], in1=E_r, op=ALU.add), outs=[b_tmp], ins=[b_tmp, B("rgc")])
                        T.op("dve", lambda e, S=S, mcol=mcol: e.scalar_tensor_tensor(out=S, in0=tmp[:], scalar=mcol, in1=S, op0=ALU.mult, op1=ALU.add),
                             outs=[b_hin], ins=[b_tmp, b_hin, b_const])
                T.barrier()
                s3.close()
                a2 = [sb(s2, f"ra2{i}", [128, NLAT]) for i in range(2)]
                g2 = [sb(s2, f"rg2{i}", [128, NLAT]) for i in range(2)]
                hf = sb(s2, "rhf", [128, NLAT])
                hr = sb(s2, "rhr", [128, NLAT])
                gtl = sb(s2, "rgtl", [128, NLAT], BF16)
                mol = sb(s2, "rmol", [128, NLAT], BF16)
                for hd in range(8):
                    for d in range(2):
                        dh = d * 8 + hd
                        ba, bg = B(("ra2", d)), B(("rg2", d))
                        T.dma(a2[d][:], SA[dh * 128:(dh + 1) * 128, :], outs=[ba], ins=[B(("SA", dh))])
                        T.dma(g2[d][:], SG[dh * 128:(dh + 1) * 128, :], outs=[bg], ins=[B(("SG", dh))])
                        if d == 0:
                            T.op("dve", lambda e, dh=dh: e.tensor_tensor_scan(out=hf[:], data0=a2[0][:], data1=g2[0][:], initial=hin[:, dh:dh + 1], op0=ALU.mult, op1=ALU.add),
                                 outs=[B("rhf")], ins=[ba, bg, b_hin])
                        else:
                            T.op("dve", lambda e, dh=dh: e.tensor_tensor_scan(out=hr[:, ::-1], data0=a2[1][:, ::-1], data1=g2[1][:, ::-1], initial=hin[:, dh:dh + 1],
                                                                              op0=ALU.mult, op1=ALU.add), outs=[B("rhr")], ins=[ba, bg, b_hin])
                    T.dma(gtl[:], GR[hd * 128:(hd + 1) * 128, OWN0:OWN0 + NLAT], outs=[B("rgtl")], ins=[B(("GR", hd))])
                    T.op("dve", lambda e: e.tensor_tensor(out=hf[:], in0=hf[:], in1=hr[:], op=ALU.add), outs=[B("rhf")], ins=[B("rhf"), B("rhr")])
                    T.op("dve", lambda e: e.tensor_tensor(out=mol[:], in0=hf[:], in1=gtl[:], op=ALU.mult), outs=[B("rmol")], ins=[B("rhf"), B("rgtl")])
                    T.dma(MIX[(8 + hd) * 128:(9 + hd) * 128, NCTX:NMIX], mol[:], outs=[B(("MIX", 8 + hd))], ins=[B("rmol")], q="pool")
                T.barrier()
        if mode == "A":
            return nc

        with ExitStack() as s1:
            qt = [sb(s1, f"aq{i}", [128, NTK], BF16) for i in range(2)]
            kt = [sb(s1, f"ak{i}", [128, NTK], BF16) for i in range(2)]
            vt = [sb(s1, f"avt{i}", [128, NTK], BF16) for i in range(2)]
            gct = [sb(s1, f"agc{i}", [128, NTK], BF16) for i in range(2)]
            tf = [sb(s1, f"atf{i}", [128, 896]) for i in range(2)]
            vv = sb(s1, "avv", [128, 22, 128], BF16)
            tb = sb(s1, "atb", [128, 5 * 896])
            pp_ = [sb(s1, f"app{i}", [128, 768]) for i in range(2)]
            pT = [sb(s1, f"apT{i}", [128, 1024], BF16) for i in range(2)]
            rc = [sb(s1, f"arc{i}", [128, 128]) for i in range(2)]
            of = [sb(s1, f"aof{i}", [128, 128]) for i in range(2)]
            om = sb(s1, "aom", [128, NMIX], BF16)
            psS = [ps(s1, f"aps{i}", [128, 1024]) for i in range(2)]
            psO = [ps(s1, f"apo{i}", [128, 512]) for i in range(2)]
            psV = [ps(s1, f"apv{i}", [128, 1024], BF16) for i in range(2)]
            scale = 128.0 ** -0.5
            n = 0
            for h in range(16):
                i = h % 2
                bq, bk, bvt, bgc, btf = B(("aq", i)), B(("ak", i)), B(("avt", i)), B(("agc", i)), B(("atf", i))
                T.dma(qt[i][:], QT[h * 128:(h + 1) * 128, :], outs=[bq], ins=[B(("QT", h))])
                T.dma(kt[i][:], KT[h * 128:(h + 1) * 128, :], outs=[bk], ins=[B(("KT", h))])
                T.dma(vt[i][:], VT[h * 128:(h + 1) * 128, :], outs=[bvt], ins=[B(("VT", h))])
                T.dma(gct[i][:], GC[h * 128:(h + 1) * 128, :], outs=[bgc], ins=[B(("GC", h))])
                T.dma(tf[i][:], tfd[h * 128:(h + 1) * 128, :], outs=[btf])
                b_tb = B("atb")
                for v in range(5):
                    T.op("pool", lambda e, v=v, i=i: e.tensor_tensor(out=tb[:, v * 896:(v + 1) * 896], in0=tf[i][:], in1=ms[:, v * 896:(v + 1) * 896], op=ALU.add),
                         outs=[b_tb], ins=[btf, b_const])
                b_vv = B("avv")
                for g in range(0, 22, 8):
                    ng = min(8, 22 - g)
                    j = (g // 8) % 2
                    bpv = B(("apv", j))
                    T.mm([lambda e, g=g, jj=jj, j=j, i=i: e.transpose(psV[j][:, jj * 128:(jj + 1) * 128], vt[i][:, (g + jj) * 128:(g + jj + 1) * 128], identb[:])
                          for jj in range(ng)], outs=[bpv], ins=[bvt, b_const])
                    T.op("act", lambda e, g=g, ng=ng, j=j: e.activation(out=vv[:, g:g + ng, :], in_=psV[j][:, 0:ng * 128].rearrange("p (a b) -> p a b", b=128), func=AF.Copy),
                         outs=[b_vv], ins=[bpv])
                b_om = B("aom")
                blocks = [("c", 0), ("c", 1)] if not last else []
                blocks += [("l", b) for b in range(16)]
                for (bt, b) in blocks:
                    si = n % 2
                    n += 1
                    bS, bO, bpp, bpT, brc, bof = B(("aps", si)), B(("apo", si)), B(("app", si)), B(("apT", si)), B(("arc", si)), B(("aof", si))
                    if bt == "c":
                        q0, m0 = b * 128, b * 128
                        ktiles, var = [], None
                    else:
                        q0, m0 = OWN0 + b * 128, NCTX + b * 128
                        if b == 0:
                            ktiles, var = list(range(0, 6)), 1
                        elif b == 15:
                            ktiles, var = list(range(-1, 5)), 4
                        else:
                            ktiles, var = list(range(0, 5)), {1: 2, 14: 3}.get(b, 0)
                    nl = len(ktiles)
                    fns = []
                    for ti_, kt_ in enumerate(ktiles):
                        k0 = NCTX + (b + kt_) * 128
                        fns.append(lambda e, ti_=ti_, k0=k0, q0=q0, si=si, i=i: e.matmul(psS[si][:, ti_ * 128:(ti_ + 1) * 128], lhsT=kt[i][:, k0:k0 + 128], rhs=qt[i][:, q0:q0 + 128],
                                                                                         start=True, stop=True))
                    for cj in range(2):
                        fns.append(lambda e, cj=cj, q0=q0, si=si, i=i: e.matmul(psS[si][:, 768 + cj * 128:896 + cj * 128], lhsT=kt[i][:, cj * 128:(cj + 1) * 128], rhs=qt[i][:, q0:q0 + 128],
                                                                                start=True, stop=True))
                    T.mm(fns, outs=[bS], ins=[bq, bk])
                    if nl:
                        tcol = var * 896 + ((ktiles[0] + 1) * 128)
                        T.op("dve", lambda e, si=si, nl=nl, tcol=tcol: e.scalar_tensor_tensor(out=pp_[si][:, 0:nl * 128], in0=psS[si][:, 0:nl * 128], scalar=scale,
                                                                                             in1=tb[:, tcol:tcol + nl * 128], op0=ALU.mult, op1=ALU.add),
                             outs=[bpp], ins=[bS, b_tb])
                        T.op("act", lambda e, si=si, nl=nl: e.activation(out=pT[si][:, 0:nl * 128], in_=pp_[si][:, 0:nl * 128], func=AF.Exp), outs=[bpT], ins=[bpp])
                    T.op("act", lambda e, si=si: e.activation(out=pT[si][:, 768:1024], in_=psS[si][:, 768:1024], func=AF.Exp, scale=scale), outs=[bpT], ins=[bS])
                    tl = [(ti_ * 128, 2 + (b + kt_)) for ti_, kt_ in enumerate(ktiles)] + [(768, 0), (896, 1)]
                    fns = []
                    for m_, (c_, vi) in enumerate(tl):
                        fns.append(lambda e, m_=m_, c_=c_, vi=vi, si=si: e.matmul(psO[si][:, 0:128], lhsT=vv[:, vi, :], rhs=pT[si][:, c_:c_ + 128], start=(m_ == 0), stop=(m_ == len(tl) - 1)))
                    for m_, (c_, vi) in enumerate(tl):
                        fns.append(lambda e, m_=m_, c_=c_, si=si: e.matmul(psO[si][:, 128:256], lhsT=onesb[:], rhs=pT[si][:, c_:c_ + 128], start=(m_ == 0), stop=(m_ == len(tl) - 1)))
                    T.mm(fns, outs=[bO], ins=[bpT, b_vv, b_const])
                    T.op("dve", lambda e, si=si: e.reciprocal(out=rc[si][:], in_=psO[si][:, 128:256]), outs=[brc], ins=[bO])
                    T.op("dve", lambda e, si=si: e.tensor_tensor(out=of[si][:], in0=psO[si][:, 0:128], in1=rc[si][:], op=ALU.mult), outs=[bof], ins=[bO, brc])
                    T.op("pool", lambda e, si=si, q0=q0, m0=m0, i=i: e.tensor_tensor(out=om[:, m0:m0 + 128], in0=of[si][:], in1=gct[i][:, q0:q0 + 128], op=ALU.mult),
                         outs=[b_om], ins=[bof, bgc])
                c0 = NCTX if last else 0
                T.dma(MIX[(16 + h) * 128:(17 + h) * 128, c0:NMIX], om[:, c0:NMIX], outs=[B(("MIX", 16 + h))], ins=[b_om], q="pool")
            T.barrier()

        Xdst = XT if last else xo
        MTT = [(0, 256), (256, 512), (768, 512), (1280, 512), (1792, 512)]
        XCOL = [0, OWN0, OWN0 + 512, OWN0 + 1024, OWN0 + 1536]
        for gi, grp in enumerate([[0, 1, 2], [3, 4]]):
            g0 = MTT[grp[0]][0]
            gn = sum(MTT[t][1] for t in grp)
            with ExitStack() as s1:
                mT = sb(s1, "mT", [128, KC, gn], BF16)
                b_mT = B(("mT", gi))
                T.dma(mT[:], MIX.rearrange("(c p) t -> p c t", p=128)[:, :, g0:g0 + gn], outs=[b_mT], ins=[B(("MIX", c)) for c in range(KC)])
                xo_ = [sb(s1, f"ox{i}", [128, 512]) for i in range(3)]
                ok = {"k": 0}

                def evac_out(chunk, tidx, nt, pt, bps):
                    if last and tidx == 0:
                        return
                    t0 = MTT[tidx][0]
                    i = ok["k"] % 3
                    ok["k"] += 1
                    bx = B(("ox", i))
                    j = 1 if tidx == 0 else 0
                    T.dma(xo_[i][:, 0:nt], xT[chunk * 128:(chunk + 1) * 128, XCOL[tidx]:XCOL[tidx] + nt], outs=[bx])
                    T.op("dve", lambda e: e.scalar_tensor_tensor(out=xo_[i][:, 0:nt], in0=pt[:, 0:nt], scalar=modT[:, j * 96 + 64 + chunk:j * 96 + 65 + chunk], in1=xo_[i][:, 0:nt],
                                                                 op0=ALU.mult, op1=ALU.add), outs=[bx], ins=[bx, bps, b_const])
                    T.dma(Xdst[chunk * 128:(chunk + 1) * 128, t0:t0 + nt], xo_[i][:, 0:nt], outs=[B(("XO", chunk))], ins=[bx], q="pool")

                tiles = [(t, MTT[t][0] - g0, MTT[t][1]) for t in grp]
                gemm(s1, w_out, KC, [c * 256 for c in range(D // 256)], mT, b_mT, tiles, evac_out, "go")
                T.barrier()

        if last:
            with ExitStack() as s1:
                xt = [sb(s1, f"fx{i}", [128, 512]) for i in range(3)]
                sq = [sb(s1, f"fsq{i}", [128, 512]) for i in range(2)]
                rstd = sb(s1, "frstd", [128, 512])
                pss = [ps(s1, f"fps{i}", [128, 512]) for i in range(2)]
                kx = 0
                b_out = B("out")
                for tidx in range(1, 5):
                    t0, nt = MTT[tidx]
                    pst_ = pss[tidx % 2]
                    bpss = B(("fps", tidx % 2))
                    for kc in range(KC):
                        xi = kx % 3
                        kx += 1
                        bxt = B(("fx", xi))
                        T.dma(xt[xi][:], XT[kc * 128:(kc + 1) * 128, t0:t0 + nt], outs=[bxt], ins=[B(("XO", kc))])
                        bsq = B(("fsq", kc % 2))
                        T.op("act", lambda e, xi=xi, kc=kc: e.activation(out=sq[kc % 2][:], in_=xt[xi][:], func=AF.Square), outs=[bsq], ins=[bxt])
                        T.mm([lambda e, kc=kc, pst_=pst_: e.matmul(pst_[:], lhsT=ones[:], rhs=sq[kc % 2][:], start=(kc == 0), stop=(kc == KC - 1))], outs=[bpss], ins=[bsq, b_const])
                    b_rstd = B("frstd")
                    T.op("dve", lambda e, pst_=pst_: e.tensor_scalar(out=rstd[:], in0=pst_[:], scalar1=1.0 / D, scalar2=EPS, op0=ALU.mult, op1=ALU.add), outs=[b_rstd], ins=[bpss])
                    T.op("act", lambda e: e.activation(out=rstd[:], in_=rstd[:], func=AF.Sqrt), outs=[b_rstd], ins=[b_rstd])
                    T.op("dve", lambda e: e.reciprocal(out=rstd[:], in_=rstd[:]), outs=[b_rstd], ins=[b_rstd])
                    for kc in range(KC):
                        xi = kx % 3
                        kx += 1
                        bxt = B(("fx", xi))
                        T.dma(xt[xi][:], XT[kc * 128:(kc + 1) * 128, t0:t0 + nt], outs=[bxt], ins=[B(("XO", kc))])
                        T.op("dve", lambda e, xi=xi, kc=kc: e.scalar_tensor_tensor(out=xt[xi][:], in0=xt[xi][:], scalar=fgt[:, kc:kc + 1], in1=rstd[:], op0=ALU.mult, op1=ALU.mult),
                             outs=[bxt], ins=[bxt, b_rstd, b_const])
                        T.dma(outT[kc * 128:(kc + 1) * 128, t0 - NCTX:t0 - NCTX + nt], xt[xi][:], outs=[b_out], ins=[bxt], q="pool")
                T.barrier()
    return nc


def _ms_tables(me):
    ms = np.full((5, 7, 128, 128), NEG, np.float32)
    for v, b in enumerate((5, 0, 1, 14, 15)):
        for s, ktile in enumerate(range(-1, 6)):
            for a in range(2):
                for c in range(2):
                    key_local = 2 * b + 2 * ktile + a
                    r = 32 * me + 2 * b + c
                    rs = min(max(r - 4, 0), 248)
                    kg = 32 * me + key_local - 4
                    if rs <= kg <= rs + 7:
                        ms[v, s, a * 64:(a + 1) * 64, c * 64:(c + 1) * 64] = 0.0
    return np.ascontiguousarray(ms.transpose(2, 0, 1, 3).reshape(128, 5 * 7 * 128))


def _tf_tables(rpb):
    L, H = rpb.shape[0], rpb.shape[1]
    cols = np.arange(64)
    cs = np.clip(cols - 8, 0, 48)
    kc_, qc_ = np.meshgrid(cols, cols, indexing="ij")
    inwin = (kc_ >= cs[None, :]) & (kc_ < cs[None, :] + 16)
    dc = np.clip(kc_ - qc_ + 15, 0, 30)
    tf = np.full((L, H, 2, 64, 7, 2, 64), NEG, np.float32)
    for s, ktile in enumerate(range(-1, 6)):
        for a in range(2):
            for c in range(2):
                dr = 2 * ktile + a - 4 - c
                if -7 <= dr <= 7:
                    blk = rpb[:, :, dr + 7, :][:, :, dc]
                    tf[:, :, a, :, s, c, :] = np.where(inwin[None, None], blk, NEG)
    return np.ascontiguousarray(tf.reshape(L, H * 128, 7 * 128))


def _fm(v):
    return np.ascontiguousarray(v.reshape(-1, 128).T)


def _pack_params(norm_g, conv_w, conv_b, ln_g, ln_b, b_pw, rconv_w, rconv_b, b_r, b_i, lam):
    pp = np.zeros((DEPTH, 128, NPP), np.float32)
    for l in range(DEPTH):
        pp[l, :, PP_NG:PP_NG + 32] = _fm(norm_g[l])
        pp[l, :, PP_CW:PP_CW + 248] = conv_w[l].reshape(31, 8, 128).transpose(2, 1, 0).reshape(128, 248)
        pp[l, :, PP_CB:PP_CB + 8] = _fm(conv_b[l])
        pp[l, :, PP_LG:PP_LG + 8] = _fm(ln_g[l])
        pp[l, :, PP_LB:PP_LB + 8] = _fm(ln_b[l])
        pp[l, :, PP_BPW:PP_BPW + 8] = _fm(b_pw[l])
        pp[l, :, PP_RW:PP_RW + 64] = rconv_w[l].reshape(2, 4, 8, 128).transpose(3, 0, 2, 1).reshape(128, 64)
        pp[l, :, PP_RB:PP_RB + 16] = rconv_b[l].reshape(16, 128).T
        pp[l, :, PP_BR:PP_BR + 16] = b_r[l].reshape(16, 128).T
        pp[l, :, PP_BI:PP_BI + 16] = b_i[l].reshape(16, 128).T
        pp[l, :, PP_LAM:PP_LAM + 16] = lam[l].reshape(16, 128).T
    return pp


_PROG = {}


def _prog(name):
    if name not in _PROG:
        _PROG[name] = build_mod() if name == "M" else build_layer(name)
    return _PROG[name]


def _launch(name, in_maps):
    return run_bass_kernel_spmd(_prog(name), in_maps, core_ids=list(range(NC))).results


def kernel(x, c, ctx, c_ctx, w_ada, b_ada, norm_g, w_in, conv_w, conv_b, ln_g, ln_b, w_pw, b_pw,
           rconv_w, rconv_b, w_r, b_r, w_i, b_i, lam, rpb, w_out, final_g, _n_layers=DEPTH, _debug=None):
    f = np.float32
    A = lambda v: np.asarray(v, dtype=f)
    x, c, ctx, c_ctx, w_ada, b_ada, w_in, w_out, w_pw = A(x), A(c), A(ctx), A(c_ctx), A(w_ada), A(b_ada), A(w_in), A(w_out), A(w_pw)
    pp = _pack_params(A(norm_g), A(conv_w), A(conv_b), A(ln_g), A(ln_b), A(b_pw), A(rconv_w), A(rconv_b), A(b_r), A(b_i), A(lam))
    w_ri = np.ascontiguousarray(np.stack([A(w_r), A(w_i)], axis=2).reshape(DEPTH, 32 * 128, 128))
    tf = _tf_tables(A(rpb))
    ident = np.eye(128, dtype=f)
    fgt = _fm(A(final_g))
    cvec = np.ascontiguousarray(np.concatenate([_fm(c.reshape(-1)), _fm(c_ctx.reshape(-1))], axis=1))
    res = _launch("M", [{
        "cvec": cvec,
        "w_ada_sh": np.ascontiguousarray(w_ada[:, :, me * ADA_SH:(me + 1) * ADA_SH].reshape(DEPTH * D, ADA_SH)),
        "b_ada_sh": np.ascontiguousarray(np.broadcast_to(b_ada[:, me * ADA_SH:(me + 1) * ADA_SH].reshape(1, -1), (2, DEPTH * ADA_SH))),
    } for me in range(NC)])
    modt = []
    for l in range(DEPTH):
        cols = []
        for j in range(2):
            m = np.concatenate([res[r]["modrow"][j, l * ADA_SH:(l + 1) * ADA_SH] for r in range(NC)])
            cols.append(m.reshape(3, 32, 128).transpose(2, 0, 1).reshape(128, 96))
        modt.append(np.ascontiguousarray(np.concatenate(cols, axis=1).astype(f)))
    flags = []
    for me in range(NC):
        fl = np.zeros((128, 18), f)
        fl[:, 0] = 0.0 if me == 0 else 1.0
        fl[:, 1] = 0.0 if me == NC - 1 else 1.0
        fl[:, 2:2 + me] = 1.0
        fl[:, 10 + me + 1:18] = 1.0
        flags.append(fl)
    mss = [_ms_tables(me) for me in range(NC)]
    own = [np.ascontiguousarray(x[0, me * NLAT:(me + 1) * NLAT, :].T) for me in range(NC)]
    cx = [np.ascontiguousarray(ctx[0].T) for _ in range(NC)]
    out = None
    for l in range(_n_layers):
        last = (l == DEPTH - 1)
        xe = []
        for me in range(NC):
            t = np.zeros((D, NTK), f)
            t[:, 0:NCTX] = cx[me]
            if me > 0:
                t[:, NCTX:OWN0] = own[me - 1][:, NLAT - 256:]
            t[:, OWN0:OWN0 + NLAT] = own[me]
            if me < NC - 1:
                t[:, OWN0 + NLAT:] = own[me + 1][:, :256]
            xe.append(t)
        w_rx = np.ascontiguousarray(w_in[l][:, 3072:4096])
        base = [{"xT": xe[me], "pp": pp[l], "modt": modt[l], "w_ri": w_ri[l], "flags": flags[me]} for me in range(NC)]
        resA = _launch("A", [dict(b, w_rx=w_rx) for b in base])
        gcar = np.ascontiguousarray(np.concatenate([resA[r]["car"] for r in range(NC)], axis=1))
        extra = {"w_in": w_in[l], "w_out": w_out[l], "w_pw": w_pw[l], "tf": tf[l], "gcar": gcar, "ident": ident}
        if last:
            extra["final_g_t"] = fgt
        resB = _launch("L" if last else "B", [dict(base[me], ms=mss[me], **extra) for me in range(NC)])
        if _debug is not None:
            _debug.append((resA, resB))
        if last:
            out = np.empty((1, NC * NLAT, D), f)
            for me in range(NC):
                out[0, me * NLAT:(me + 1) * NLAT, :] = resB[me]["outT"].T
        else:
            own = [np.ascontiguousarray(resB[me]["xo"][:, NCTX:]) for me in range(NC)]
            cx = [np.ascontiguousarray(resB[me]["xo"][:, :NCTX]) for me in range(NC)]
    return out
```

```python
import numpy as np
from contextlib import ExitStack
import concourse.bass as bass
import concourse.mybir as mybir
from concourse.bass_utils import run_bass_kernel_spmd

F32 = mybir.dt.float32
BF16 = mybir.dt.bfloat16
AF = mybir.ActivationFunctionType
ALU = mybir.AluOpType
AX = mybir.AxisListType

NC = 8
D = 4096
KC = 32
NCTX = 256
NLAT = 2048
NIN = 13312
DEPTH = 4
NEG = -30000.0
EPS = 1e-6
NTK = 2816
OWN0 = 512
NMIX = NCTX + NLAT
TT = [(0, 256), (256, 256), (512, 512), (1024, 512), (1536, 512), (2048, 512), (2560, 256)]
GROUPS = [[0, 1, 2], [3, 4], [5, 6]]
MT = [0, 2, 3, 4, 5]
C_AVAL, C_AGLU, C_AGATE, C_RX, C_RGATE, C_Q, C_K, C_V, C_CGATE = 0, 8, 16, 24, 32, 40, 56, 72, 88
PP_NG, PP_CW, PP_CB, PP_LG, PP_LB, PP_BPW, PP_RW, PP_RB, PP_BR, PP_BI, PP_LAM = 0, 32, 280, 288, 296, 304, 312, 376, 392, 408, 424
NPP = 440
ADA_SH = 1536


class Buf:
    __slots__ = ("w", "r")

    def __init__(self):
        self.w = None
        self.r = {}


class Tracker:
    def __init__(self, nc, es):
        self.nc = nc
        self.eng = {"pe": nc.tensor, "act": nc.scalar, "dve": nc.vector, "pool": nc.gpsimd, "sp": nc.sync}
        self.sem = {k: es.enter_context(nc.semaphore("s_" + k)) for k in ("pe", "act", "dve", "pool")}
        self.cnt = {k: 0 for k in self.sem}
        self.dq = {}
        for q, n in (("sp", 12), ("pool", 8)):
            self.dq[q] = {"sems": [es.enter_context(nc.semaphore(f"d_{q}{i}")) for i in range(n)],
                          "cnt": [0] * n, "nxt": 0}
        self.csem = es.enter_context(nc.semaphore("s_cc"))
        self.ccnt = 0
        self.waited = {k: {} for k in self.eng}
        self.bufs = {}

    def buf(self, key):
        b = self.bufs.get(key)
        if b is None:
            b = self.bufs[key] = Buf()
        return b

    def _wait(self, eng, ev):
        sem, val = ev
        w = self.waited[eng]
        k = id(sem)
        if w.get(k, 0) >= val:
            return
        self.eng[eng].wait_ge(sem, val)
        w[k] = val

    def _deps(self, eng, outs, ins):
        for b in ins:
            if b.w is not None:
                self._wait(eng, b.w)
        for b in outs:
            if b.w is not None:
                self._wait(eng, b.w)
            for ev in b.r.values():
                self._wait(eng, ev)

    def _post(self, ev, outs, ins):
        k = id(ev[0])
        for b in outs:
            b.w = ev
            b.r = {}
        for b in ins:
            b.r[k] = ev

    def op(self, eng, fn, outs=(), ins=()):
        self._deps(eng, outs, ins)
        ins_ = fn(self.eng[eng])
        self.cnt[eng] += 1
        ins_.then_inc(self.sem[eng], 1)
        ev = (self.sem[eng], self.cnt[eng])
        self._post(ev, outs, ins)
        return ev

    def mm(self, fns, outs, ins):
        self._deps("pe", outs, ins)
        last = None
        for fn in fns:
            last = fn(self.nc.tensor)
        self.cnt["pe"] += 1
        last.then_inc(self.sem["pe"], 1)
        ev = (self.sem["pe"], self.cnt["pe"])
        self._post(ev, outs, ins)
        return ev

    def dma(self, out, in_, outs=(), ins=(), q="sp"):
        dq = self.dq[q]
        i = dq["nxt"] % len(dq["sems"])
        dq["nxt"] += 1
        sem = dq["sems"][i]
        if dq["cnt"][i]:
            self._wait(q, (sem, dq["cnt"][i]))
        self._deps(q, outs, ins)
        self.eng[q].dma_start(out=out, in_=in_).then_inc(sem, 16)
        dq["cnt"][i] += 16
        ev = (sem, dq["cnt"][i])
        self._post(ev, outs, ins)
        return ev

    def coll(self, in_ap, out_ap, outs, ins):
        self._deps("pool", outs, ins)
        self.nc.gpsimd.collective_compute("AllGather", ALU.bypass, replica_groups=[list(range(NC))],
                                          ins=[in_ap], outs=[out_ap]).then_inc(self.csem)
        self.ccnt += 1
        ev = (self.csem, self.ccnt)
        self._post(ev, outs, ins)
        return ev

    def barrier(self):
        evs = [(self.sem[k], self.cnt[k]) for k in self.sem if self.cnt[k]]
        for dq in self.dq.values():
            evs += [(s, c) for s, c in zip(dq["sems"], dq["cnt"]) if c]
        if self.ccnt:
            evs.append((self.csem, self.ccnt))
        for e in self.eng:
            for ev in evs:
                self._wait(e, ev)


def _mk(nc, es):
    T = Tracker(nc, es)
    uid = [0]

    def sb(s, name, shape, dtype=F32):
        uid[0] += 1
        return s.enter_context(nc.sbuf_tensor(f"sb{uid[0]}_{name}", shape, dtype))

    def ps(s, name, shape, dtype=F32):
        uid[0] += 1
        return s.enter_context(nc.psum_tensor(f"ps{uid[0]}_{name}", shape, dtype))

    return T, sb, ps


def build_mod():
    nc = bass.Bass("TRN2", target_bir_lowering=False)
    dt = lambda name, shape, dtype=F32: nc.dram_tensor(name, shape, dtype, kind="ExternalInput").ap()
    cvec = dt("cvec", [128, 64])
    w_ada = dt("w_ada_sh", [DEPTH * D, ADA_SH])
    b_ada = dt("b_ada_sh", [2, DEPTH * ADA_SH])
    out = nc.dram_tensor("modrow", [2, DEPTH * ADA_SH], F32, kind="ExternalOutput").ap()
    with ExitStack() as es:
        T, sb, ps = _mk(nc, es)
        B = T.buf
        cs = sb(es, "cs", [128, 64])
        cs2 = sb(es, "cs2", [128, KC, 2])
        stg = [sb(es, f"astg{i}", [128, 8, 512]) for i in range(3)]
        modrow = sb(es, "modrow", [2, DEPTH * ADA_SH])
        brow = sb(es, "brow", [2, DEPTH * ADA_SH])
        pm = [ps(es, f"pm{i}", [128, 512]) for i in range(2)]
        b_cs, b_mr = B("cs"), B("modrow")
        T.dma(cs[:], cvec, outs=[b_cs])
        T.dma(brow[:], b_ada, outs=[b_mr])
        T.op("act", lambda e: e.activation(out=cs2[:, :, 0], in_=cs[:, 0:32], func=AF.Silu), outs=[b_cs], ins=[b_cs])
        T.op("act", lambda e: e.activation(out=cs2[:, :, 1], in_=cs[:, 32:64], func=AF.Silu), outs=[b_cs], ins=[b_cs])
        k = 0
        for l in range(DEPTH):
            for ct in range(3):
                pmt = pm[(l * 3 + ct) % 2]
                bpm = B(("pm", (l * 3 + ct) % 2))
                for qd in range(4):
                    st = stg[k % 3]
                    bst = B(("astg", k % 3))
                    k += 1
                    r0 = l * D + qd * 1024
                    T.dma(st[:], w_ada[r0:r0 + 1024, ct * 512:(ct + 1) * 512].rearrange("(k p) c -> p k c", p=128), outs=[bst])
                    T.mm([lambda e, st=st, kk=kk, qd=qd, pmt=pmt: e.matmul(pmt[0:2, :], lhsT=cs2[:, qd * 8 + kk, :], rhs=st[:, kk, :],
                                                                         start=(qd == 0 and kk == 0), stop=(qd == 3 and kk == 7))
                          for kk in range(8)], outs=[bpm], ins=[bst, b_cs])
                c0 = l * ADA_SH + ct * 512
                T.op("dve", lambda e, pmt=pmt, c0=c0: e.tensor_tensor(out=modrow[:, c0:c0 + 512], in0=pmt[0:2, :], in1=brow[:, c0:c0 + 512], op=ALU.add),
                     outs=[b_mr], ins=[bpm, b_mr])
        T.dma(out, modrow[:], outs=[B("out")], ins=[b_mr], q="pool")
        T.barrier()
    return nc


def build_layer(mode):
    last = (mode == "L")
    nc = bass.Bass("TRN2", target_bir_lowering=False)
    dt = lambda name, shape, dtype=F32: nc.dram_tensor(name, shape, dtype, kind="ExternalInput").ap()
    sc = lambda name, shape, dtype=F32: nc.dram_tensor(name, shape, dtype).ap()
    xT = dt("xT", [KC * 128, NTK])
    ppd = dt("pp", [128, NPP])
    modt = dt("modt", [128, 192])
    w_ri = dt("w_ri", [32 * 128, 128])
    flagd = dt("flags", [128, 18])
    if mode == "A":
        w_in = dt("w_rx", [D, 1024])
        car_out = nc.dram_tensor("car", [128, 32], F32, kind="ExternalOutput").ap()
    else:
        w_in = dt("w_in", [D, NIN])
        w_out = dt("w_out", [D, D])
        w_pw = dt("w_pw", [1024, 1024])
        tfd = dt("tf", [16 * 128, 7 * 128])
        msd = dt("ms", [128, 5 * 7 * 128])
        gcard = dt("gcar", [128, NC * 32])
        identd = dt("ident", [128, 128])
        if last:
            fg = dt("final_g_t", [128, KC])
            outT = nc.dram_tensor("outT", [KC * 128, NLAT], F32, kind="ExternalOutput").ap()
        else:
            xo = nc.dram_tensor("xo", [KC * 128, NMIX], F32, kind="ExternalOutput").ap()
    RX = sc("RX", [8 * 128, NTK])
    SA = sc("SA", [16 * 128, NLAT])
    SG = sc("SG", [16 * 128, NLAT])
    if mode != "A":
        XT = sc("XT", [KC * 128, NMIX])
        ZV = sc("ZV", [8 * 128, NTK], BF16)
        ZS = sc("ZS", [8 * 128, NTK], BF16)
        GA = sc("GA", [8 * 128, NTK], BF16)
        GR = sc("GR", [8 * 128, NTK], BF16)
        QT = sc("QT", [16 * 128, NTK], BF16)
        KT = sc("KT", [16 * 128, NTK], BF16)
        VT = sc("VT", [16 * 128, NTK], BF16)
        GC = sc("GC", [16 * 128, NTK], BF16)
        MIX = sc("MIX", [KC * 128, NMIX], BF16)

    with ExitStack() as es:
        T, sb, ps = _mk(nc, es)
        B = T.buf
        ones = sb(es, "ones", [128, 128])
        modT = sb(es, "modT", [128, 192])
        gmod = sb(es, "gmod", [128, 64])
        pp = sb(es, "pp", [128, NPP])
        clam = sb(es, "clam", [128, 32])
        flags = sb(es, "flags", [128, 18])
        b_const = B("const")
        T.dma(pp[:], ppd, outs=[b_const])
        T.dma(modT[:], modt, outs=[b_const])
        T.dma(flags[:], flagd, outs=[b_const])
        T.op("dve", lambda e: e.memset(ones[:], 1.0), outs=[b_const])
        if mode != "A":
            ident = sb(es, "ident", [128, 128])
            identb = sb(es, "identb", [128, 128], BF16)
            onesb = sb(es, "onesb", [128, 128], BF16)
            ms = sb(es, "ms", [128, 5 * 896])
            T.dma(ident[:], identd, outs=[b_const])
            T.dma(ms[:], msd, outs=[b_const])
            T.op("dve", lambda e: e.memset(onesb[:], 1.0), outs=[b_const])
            T.op("dve", lambda e: e.tensor_copy(out=identb[:], in_=ident[:]), outs=[b_const], ins=[b_const])
            if last:
                fgt = sb(es, "fgt", [128, KC])
                T.dma(fgt[:], fg, outs=[b_const])
        pl = lambda off, n=1: pp[:, off:off + n]
        cl_, cl2_ = clam[:, 0:16], clam[:, 16:32]
        T.op("act", lambda e: e.activation(out=cl_, in_=pl(PP_LAM, 16), func=AF.Exp, scale=-1.0), outs=[b_const], ins=[b_const])
        T.op("dve", lambda e: e.tensor_scalar(out=cl_, in0=cl_, scalar1=1.0, scalar2=None, op0=ALU.add), outs=[b_const], ins=[b_const])
        T.op("act", lambda e: e.activation(out=cl_, in_=cl_, func=AF.Ln), outs=[b_const], ins=[b_const])
        T.op("dve", lambda e: e.tensor_scalar(out=cl2_, in0=cl_, scalar1=-16.0, scalar2=None, op0=ALU.mult), outs=[b_const], ins=[b_const])
        T.op("dve", lambda e: e.tensor_scalar(out=cl_, in0=cl_, scalar1=-8.0, scalar2=None, op0=ALU.mult), outs=[b_const], ins=[b_const])
        for j in range(2):
            T.op("dve", lambda e, j=j: e.scalar_tensor_tensor(out=gmod[:, j * 32:(j + 1) * 32], in0=modT[:, j * 96 + 32:j * 96 + 64], scalar=1.0,
                                                              in1=pl(PP_NG, 32), op0=ALU.add, op1=ALU.mult), outs=[b_const], ins=[b_const])
        T.barrier()

        def gemm(s, Wd, nK, coltiles, rhs, b_rhs, tiles, evac, tag, skip=None):
            nq = nK // 8
            wb = [sb(s, f"{tag}wb{i}", [128, nK, 256], BF16) for i in range(2)]
            stg = [sb(s, f"{tag}stg{i}", [128, 8, 256]) for i in range(3)]
            pst = [ps(s, f"{tag}ps{i}", [128, 512]) for i in range(4)]
            st = {"k": 0, "p": 0}

            def load(ti):
                c0 = coltiles[ti]
                for qd in range(nq):
                    i = st["k"] % 3
                    st["k"] += 1
                    bst = B((tag, "stg", i))
                    r0 = qd * 1024
                    T.dma(stg[i][:], Wd[r0:r0 + 1024, c0:c0 + 256].rearrange("(k p) c -> p k c", p=128), outs=[bst])
                    ce = ("act", "dve", "act", "pool")[qd % 4]
                    if ce == "act":
                        T.op("act", lambda e, i=i, qd=qd, ti=ti: e.activation(out=wb[ti % 2][:, qd * 8:(qd + 1) * 8, :], in_=stg[i][:], func=AF.Copy),
                             outs=[B((tag, "wb", ti % 2, qd))], ins=[bst])
                    else:
                        T.op(ce, lambda e, i=i, qd=qd, ti=ti: e.tensor_copy(out=wb[ti % 2][:, qd * 8:(qd + 1) * 8, :], in_=stg[i][:]),
                             outs=[B((tag, "wb", ti % 2, qd))], ins=[bst])

            load(0)
            for ti in range(len(coltiles)):
                if ti + 1 < len(coltiles):
                    load(ti + 1)
                wbufs = [B((tag, "wb", ti % 2, qd)) for qd in range(nq)]
                for ch in range(2):
                    for (tidx, t0, nt) in tiles:
                        if skip is not None and skip(coltiles[ti] // 128 + ch, tidx):
                            continue
                        pi = st["p"] % 4
                        st["p"] += 1
                        bps = B((tag, "ps", pi))
                        T.mm([lambda e, kk=kk, pi=pi, ti=ti, ch=ch, t0=t0, nt=nt: e.matmul(
                            pst[pi][:, 0:nt], lhsT=wb[ti % 2][:, kk, ch * 128:(ch + 1) * 128], rhs=rhs[:, kk, t0:t0 + nt],
                            start=(kk == 0), stop=(kk == nK - 1)) for kk in range(nK)], outs=[bps], ins=wbufs + [b_rhs])
                        evac(coltiles[ti] // 128 + ch, tidx, nt, pst[pi], bps)

        for gi, grp in enumerate(GROUPS):
            g0 = TT[grp[0]][0]
            gn = sum(TT[t][1] for t in grp)
            with ExitStack() as s1:
                hT = sb(s1, "hT", [128, KC, gn], BF16)
                b_hT = B(("hT", gi))
                with ExitStack() as s2:
                    xt = [sb(s2, f"nx{i}", [128, 512]) for i in range(3)]
                    sq = [sb(s2, f"nsq{i}", [128, 512]) for i in range(2)]
                    rstd = sb(s2, "rstd", [128, 512])
                    pss = [ps(s2, f"nps{i}", [128, 512]) for i in range(2)]
                    kx = 0
                    for tix, t in enumerate(grp):
                        t0, nt = TT[t]
                        jj = 1 if t == 0 else 0
                        bpss = B(("nps", tix % 2))
                        pst_ = pss[tix % 2]
                        for kc in range(KC):
                            xi = kx % 3
                            kx += 1
                            bxt = B(("nx", xi))
                            T.dma(xt[xi][:, 0:nt], xT[kc * 128:(kc + 1) * 128, t0:t0 + nt], outs=[bxt])
                            bsq = B(("nsq", kc % 2))
                            T.op("act", lambda e, xi=xi, kc=kc, nt=nt: e.activation(out=sq[kc % 2][:, 0:nt], in_=xt[xi][:, 0:nt], func=AF.Square),
                                 outs=[bsq], ins=[bxt])
                            T.mm([lambda e, kc=kc, nt=nt, pst_=pst_: e.matmul(pst_[:, 0:nt], lhsT=ones[:], rhs=sq[kc % 2][:, 0:nt],
                                                                              start=(kc == 0), stop=(kc == KC - 1))], outs=[bpss], ins=[bsq, b_const])
                        b_rstd = B("rstd")
                        T.op("dve", lambda e, nt=nt, pst_=pst_: e.tensor_scalar(out=rstd[:, 0:nt], in0=pst_[:, 0:nt], scalar1=1.0 / D, scalar2=EPS,
                                                                               op0=ALU.mult, op1=ALU.add), outs=[b_rstd], ins=[bpss])
                        T.op("act", lambda e, nt=nt: e.activation(out=rstd[:, 0:nt], in_=rstd[:, 0:nt], func=AF.Sqrt), outs=[b_rstd], ins=[b_rstd])
                        T.op("dve", lambda e, nt=nt: e.reciprocal(out=rstd[:, 0:nt], in_=rstd[:, 0:nt]), outs=[b_rstd], ins=[b_rstd])
                        for kc in range(KC):
                            xi = kx % 3
                            kx += 1
                            bxt = B(("nx", xi))
                            T.dma(xt[xi][:, 0:nt], xT[kc * 128:(kc + 1) * 128, t0:t0 + nt], outs=[bxt])
                            T.op("dve", lambda e, xi=xi, nt=nt: e.tensor_tensor(out=xt[xi][:, 0:nt], in0=xt[xi][:, 0:nt], in1=rstd[:, 0:nt], op=ALU.mult),
                                 outs=[bxt], ins=[bxt, b_rstd])
                            T.op("act", lambda e, xi=xi, nt=nt, kc=kc, jj=jj, t0=t0: e.activation(
                                out=hT[:, kc, t0 - g0:t0 - g0 + nt], in_=xt[xi][:, 0:nt], func=AF.Identity,
                                scale=gmod[:, jj * 32 + kc:jj * 32 + kc + 1], bias=modT[:, jj * 96 + kc:jj * 96 + kc + 1]),
                                 outs=[b_hT], ins=[bxt, b_const])
                    T.barrier()
                with ExitStack() as s2:
                    evb = [sb(s2, f"evb{i}", [128, 512], BF16) for i in range(4)]
                    evf = [sb(s2, f"evf{i}", [128, 512]) for i in range(2)]
                    ek = {"b": 0, "f": 0, "e": 0}

                    def evac_in(chunk, tidx, nt, pt, bps):
                        t0 = TT[tidx][0]
                        if mode == "A":
                            chunk = chunk + C_RX
                        if C_RX <= chunk < C_RGATE:
                            i = ek["f"] % 2
                            ek["f"] += 1
                            bo = B(("evf", i))
                            T.op("dve", lambda e: e.tensor_copy(out=evf[i][:, 0:nt], in_=pt[:, 0:nt]), outs=[bo], ins=[bps])
                            c = chunk - C_RX
                            T.dma(RX[c * 128:(c + 1) * 128, t0:t0 + nt], evf[i][:, 0:nt], outs=[B(("RX", c))], ins=[bo], q="pool")
                            return
                        i = ek["b"] % 4
                        ek["b"] += 1
                        bo = B(("evb", i))
                        o = evb[i]
                        if chunk < C_AGLU:
                            dst, name, c, fn = ZV, "ZV", chunk - C_AVAL, None
                        elif chunk < C_AGATE:
                            dst, name, c, fn = ZS, "ZS", chunk - C_AGLU, AF.Sigmoid
                        elif chunk < C_RX:
                            dst, name, c, fn = GA, "GA", chunk - C_AGATE, AF.Silu
                        elif chunk < C_Q:
                            dst, name, c, fn = GR, "GR", chunk - C_RGATE, AF.Silu
                        elif chunk < C_K:
                            dst, name, c, fn = QT, "QT", chunk - C_Q, None
                        elif chunk < C_V:
                            dst, name, c, fn = KT, "KT", chunk - C_K, None
                        elif chunk < C_CGATE:
                            dst, name, c, fn = VT, "VT", chunk - C_V, None
                        else:
                            dst, name, c, fn = GC, "GC", chunk - C_CGATE, AF.Silu
                        if fn is not None:
                            T.op("act", lambda e: e.activation(out=o[:, 0:nt], in_=pt[:, 0:nt], func=fn), outs=[bo], ins=[bps])
                        else:
                            ek["e"] += 1
                            if ek["e"] % 2:
                                T.op("dve", lambda e: e.tensor_copy(out=o[:, 0:nt], in_=pt[:, 0:nt]), outs=[bo], ins=[bps])
                            else:
                                T.op("act", lambda e: e.activation(out=o[:, 0:nt], in_=pt[:, 0:nt], func=AF.Copy), outs=[bo], ins=[bps])
                        T.dma(dst[c * 128:(c + 1) * 128, t0:t0 + nt], o[:, 0:nt], outs=[B((name, c))], ins=[bo], q="pool")

                    tiles = [(t, TT[t][0] - g0, TT[t][1]) for t in grp]
                    ncols = 1024 if mode == "A" else NIN
                    need_halo = lambda ch: ch < C_AGATE or C_RX <= ch < C_RGATE or C_K <= ch < C_CGATE
                    skip_in = None if mode == "A" else (lambda ch, tidx: tidx in (1, 6) and not need_halo(ch))
                    gemm(s2, w_in, KC, [c * 256 for c in range(ncols // 256)], hT, b_hT, tiles, evac_in, "gi", skip=skip_in)
                    T.barrier()

        if mode != "A":
            with ExitStack() as s2:
                wst = sb(s2, "cwst", [128, 8, 256])
                wpb = sb(s2, "wpb", [128, 8, 1024], BF16)
                b_wpb = B("wpb")
                for ct in range(4):
                    T.dma(wst[:], w_pw[:, ct * 256:(ct + 1) * 256].rearrange("(k p) c -> p k c", p=128), outs=[B("cwst")])
                    T.op("pool", lambda e, ct=ct: e.tensor_copy(out=wpb[:, :, ct * 256:(ct + 1) * 256], in_=wst[:]), outs=[b_wpb], ins=[B("cwst")])
                dg = sb(s2, "cdg", [128, 8 * 31, 128], BF16)
                b_dg = B("cdg")
                for cj in range(8 * 31):
                    T.op("dve" if cj % 2 else "pool", lambda e, cj=cj: e.tensor_scalar(out=dg[:, cj, :], in0=ident[:], scalar1=pl(PP_CW + cj), scalar2=None, op0=ALU.mult),
                         outs=[b_dg], ins=[b_const])
                pc = [ps(s2, f"cpc{i}", [128, 512]) for i in range(2)]
                vb = [sb(s2, f"cvb{i}", [128, 542], BF16) for i in range(2)]
                sgb = [sb(s2, f"csb{i}", [128, 542], BF16) for i in range(2)]
                ub = [sb(s2, f"cub{i}", [128, 542], BF16) for i in range(2)]
                cvo = sb(s2, "cvo", [128, 8, 512])
                sqc = [sb(s2, f"csq{i}", [128, 512]) for i in range(2)]
                mean = sb(s2, "cmean", [128, 512])
                rs = sb(s2, "crs", [128, 512])
                u2 = sb(s2, "cu2", [128, 8, 512], BF16)
                gt = [sb(s2, f"cgt{i}", [128, 512], BF16) for i in range(2)]
                mo = [sb(s2, f"cmo{i}", [128, 512], BF16) for i in range(2)]
                p1 = ps(s2, "cp1", [128, 512])
                p2 = ps(s2, "cp2", [128, 512])
                pg = [ps(s2, f"cpg{i}", [128, 512]) for i in range(2)]
                for tidx in MT:
                    t0, nt = TT[tidx]
                    m0 = t0 if tidx == 0 else t0 - 256
                    b_cvo = [B(("cvo", c)) for c in range(8)]
                    for c in range(8):
                        i = c % 2
                        bub, bvb = B(("cub", i)), B(("cvb", i))
                        isctx = (tidx == 0)
                        a0 = 15 if isctx else 0
                        a1 = (15 + nt) if isctx else (30 + nt)
                        s0 = t0 - 15 + a0
                        T.dma(vb[i][:, a0:a1], ZV[c * 128:(c + 1) * 128, s0:s0 + a1 - a0], outs=[bvb], ins=[B(("ZV", c))])
                        T.dma(sgb[i][:, a0:a1], ZS[c * 128:(c + 1) * 128, s0:s0 + a1 - a0], outs=[bvb], ins=[B(("ZS", c))])
                        T.op("dve", lambda e, i=i, a0=a0, a1=a1: e.tensor_tensor(out=ub[i][:, a0:a1], in0=vb[i][:, a0:a1], in1=sgb[i][:, a0:a1], op=ALU.mult),
                             outs=[bub], ins=[bvb])
                        if isctx:
                            T.op("dve", lambda e, i=i: e.memset(ub[i][:, 0:15], 0.0), outs=[bub])
                            T.op("dve", lambda e, i=i, nt=nt: e.memset(ub[i][:, 15 + nt:30 + nt], 0.0), outs=[bub])
                        if tidx == 2:
                            T.op("dve", lambda e, i=i: e.tensor_scalar(out=ub[i][:, 0:15], in0=ub[i][:, 0:15], scalar1=flags[:, 0:1], scalar2=None, op0=ALU.mult),
                                 outs=[bub], ins=[bub, b_const])
                        if tidx == 5:
                            T.op("dve", lambda e, i=i, nt=nt: e.tensor_scalar(out=ub[i][:, 15 + nt:30 + nt], in0=ub[i][:, 15 + nt:30 + nt], scalar1=flags[:, 1:2], scalar2=None,
                                                                           op0=ALU.mult), outs=[bub], ins=[bub, b_const])
                        bpc = B(("cpc", i))
                        T.mm([lambda e, i=i, c=c, nt=nt, j=j: e.matmul(pc[i][:, 0:nt], lhsT=dg[:, c * 31 + j, :], rhs=ub[i][:, j:j + nt], start=(j == 0), stop=(j == 30))
                              for j in range(31)], outs=[bpc], ins=[bub, b_dg])
                        T.op("act", lambda e, i=i, c=c, nt=nt: e.activation(out=cvo[:, c, 0:nt], in_=pc[i][:, 0:nt], func=AF.Identity, bias=pl(PP_CB + c)),
                             outs=[b_cvo[c]], ins=[bpc, b_const])
                    bp1, bp2 = B("cp1"), B("cp2")
                    for c in range(8):
                        T.mm([lambda e, c=c, nt=nt: e.matmul(p1[:, 0:nt], lhsT=ones[:], rhs=cvo[:, c, 0:nt], start=(c == 0), stop=(c == 7))],
                             outs=[bp1], ins=[b_cvo[c], b_const])
                    for c in range(8):
                        bsq = B(("csq", c % 2))
                        T.op("act", lambda e, c=c, nt=nt: e.activation(out=sqc[c % 2][:, 0:nt], in_=cvo[:, c, 0:nt], func=AF.Square), outs=[bsq], ins=[b_cvo[c]])
                        T.mm([lambda e, c=c, nt=nt: e.matmul(p2[:, 0:nt], lhsT=ones[:], rhs=sqc[c % 2][:, 0:nt], start=(c == 0), stop=(c == 7))],
                             outs=[bp2], ins=[bsq, b_const])
                    b_st = B("cstat")
                    T.op("dve", lambda e, nt=nt: e.tensor_scalar(out=mean[:, 0:nt], in0=p1[:, 0:nt], scalar1=1.0 / 1024, scalar2=None, op0=ALU.mult), outs=[b_st], ins=[bp1])
                    T.op("dve", lambda e, nt=nt: e.tensor_tensor(out=rs[:, 0:nt], in0=mean[:, 0:nt], in1=mean[:, 0:nt], op=ALU.mult), outs=[b_st], ins=[b_st])
                    T.op("dve", lambda e, nt=nt: e.scalar_tensor_tensor(out=rs[:, 0:nt], in0=p2[:, 0:nt], scalar=1.0 / 1024, in1=rs[:, 0:nt], op0=ALU.mult, op1=ALU.subtract),
                         outs=[b_st], ins=[b_st, bp2])
                    T.op("dve", lambda e, nt=nt: e.tensor_scalar(out=rs[:, 0:nt], in0=rs[:, 0:nt], scalar1=EPS, scalar2=None, op0=ALU.add), outs=[b_st], ins=[b_st])
                    T.op("act", lambda e, nt=nt: e.activation(out=rs[:, 0:nt], in_=rs[:, 0:nt], func=AF.Sqrt), outs=[b_st], ins=[b_st])
                    T.op("dve", lambda e, nt=nt: e.reciprocal(out=rs[:, 0:nt], in_=rs[:, 0:nt]), outs=[b_st], ins=[b_st])
                    b_u2 = B("cu2")
                    for c in range(8):
                        T.op("dve", lambda e, c=c, nt=nt: e.tensor_tensor(out=cvo[:, c, 0:nt], in0=cvo[:, c, 0:nt], in1=mean[:, 0:nt], op=ALU.subtract),
                             outs=[b_cvo[c]], ins=[b_cvo[c], b_st])
                        T.op("dve", lambda e, c=c, nt=nt: e.tensor_tensor(out=cvo[:, c, 0:nt], in0=cvo[:, c, 0:nt], in1=rs[:, 0:nt], op=ALU.mult),
                             outs=[b_cvo[c]], ins=[b_cvo[c], b_st])
                        T.op("act", lambda e, c=c, nt=nt: e.activation(out=u2[:, c, 0:nt], in_=cvo[:, c, 0:nt], func=AF.Silu, scale=pl(PP_LG + c), bias=pl(PP_LB + c)),
                             outs=[b_u2], ins=[b_cvo[c], b_const])
                    for co in range(8):
                        i = co % 2
                        bpg, bgt, bmo = B(("cpg", i)), B(("cgt", i)), B(("cmo", i))
                        T.dma(gt[i][:, 0:nt], GA[co * 128:(co + 1) * 128, t0:t0 + nt], outs=[bgt], ins=[B(("GA", co))])
                        T.mm([lambda e, ci=ci, co=co, i=i, nt=nt: e.matmul(pg[i][:, 0:nt], lhsT=wpb[:, ci, co * 128:(co + 1) * 128], rhs=u2[:, ci, 0:nt],
                                                                          start=(ci == 0), stop=(ci == 7)) for ci in range(8)], outs=[bpg], ins=[b_u2, b_wpb])
                        T.op("dve", lambda e, i=i, co=co, nt=nt: e.scalar_tensor_tensor(out=mo[i][:, 0:nt], in0=pg[i][:, 0:nt], scalar=pl(PP_BPW + co), in1=gt[i][:, 0:nt],
                                                                                     op0=ALU.add, op1=ALU.mult), outs=[bmo], ins=[bpg, bgt, b_const])
                        T.dma(MIX[co * 128:(co + 1) * 128, m0:m0 + nt], mo[i][:, 0:nt], outs=[B(("MIX", co))], ins=[bmo], q="pool")
                T.barrier()

        with ExitStack() as s2:
            wst = sb(s2, "rwst", [128, 32, 128])
            wrb = sb(s2, "wrb", [128, 32, 128], BF16)
            b_wrb = B("wrb")
            T.dma(wst[:], w_ri.rearrange("(m p) c -> p m c", p=128), outs=[B("rwst")])
            T.op("pool", lambda e: e.tensor_copy(out=wrb[:], in_=wst[:]), outs=[b_wrb], ins=[B("rwst")])
            pr = [ps(s2, f"rp{i}", [128, 512]) for i in range(4)]
            ybc = sb(s2, "rybc", [128, 8, NCTX])
            car = sb(s2, "rcar", [128, 32])
            ectx = sb(s2, "rectx", [128, 16])
            rsum = sb(s2, "rsum", [128, 1])
            gtc = sb(s2, "rgtc", [128, 8, NCTX], BF16)
            moc = sb(s2, "rmoc", [128, 8, NCTX], BF16)
            gc_ = sb(s2, "rgc", [128, 8, 32])
            hin = sb(s2, "rhin", [128, 16])
            tmp = sb(s2, "rtmp", [128, 8])
            s3 = ExitStack()
            xb = sb(s3, "rxb", [128, 6 + NCTX + 6 + NLAT])
            LB = NCTX + 6
            xc = sb(s3, "rxc", [128, NMIX])
            xcb = sb(s3, "rxcb", [128, NMIX], BF16)
            rr = sb(s3, "rr", [128, NMIX])
            ii = sb(s3, "ri", [128, NMIX])
            aa = sb(s3, "ra", [128, NMIX])
            gg = sb(s3, "rg", [128, NMIX])
            hs = sb(s3, "rhs", [128, NMIX])
            b_car, b_ybc, b_ectx = B("car"), B("ybc"), B("ectx")
            MTT = [(0, 256), (256, 512), (768, 512), (1280, 512), (1792, 512)]
            pk_ = 0
            for hd in range(8):
                b_xb = B("rxb")
                T.op("dve", lambda e: e.memset(xb[:, 0:NCTX + 6], 0.0), outs=[b_xb])
                T.dma(xb[:, 3:3 + NCTX], RX[hd * 128:(hd + 1) * 128, 0:NCTX], outs=[b_xb], ins=[B(("RX", hd))])
                T.dma(xb[:, LB:LB + 6 + NLAT], RX[hd * 128:(hd + 1) * 128, OWN0 - 3:OWN0 + NLAT + 3], outs=[b_xb], ins=[B(("RX", hd))])
                T.op("dve", lambda e: e.tensor_scalar(out=xb[:, LB:LB + 3], in0=xb[:, LB:LB + 3], scalar1=flags[:, 0:1], scalar2=None, op0=ALU.mult),
                     outs=[b_xb], ins=[b_xb, b_const])
                T.op("dve", lambda e: e.tensor_scalar(out=xb[:, LB + 3 + NLAT:LB + 6 + NLAT], in0=xb[:, LB + 3 + NLAT:LB + 6 + NLAT], scalar1=flags[:, 1:2], scalar2=None,
                                                      op0=ALU.mult), outs=[b_xb], ins=[b_xb, b_const])
                for d in range(2):
                    dh = d * 8 + hd
                    off = 0 if d == 0 else 3
                    rw = lambda j: pl(PP_RW + dh * 4 + j)
                    b_xc = B("rxc")
                    for (o0, src0, n) in ((0, off, NCTX), (NCTX, LB + off, NLAT)):
                        T.op("dve", lambda e, o0=o0, src0=src0, n=n: e.tensor_scalar(out=xc[:, o0:o0 + n], in0=xb[:, src0:src0 + n], scalar1=rw(0), scalar2=pl(PP_RB + dh),
                                                                                  op0=ALU.mult, op1=ALU.add), outs=[b_xc], ins=[b_xb, b_const])
                        for j in range(1, 4):
                            T.op("dve", lambda e, o0=o0, src0=src0, n=n, j=j: e.scalar_tensor_tensor(out=xc[:, o0:o0 + n], in0=xb[:, src0 + j:src0 + j + n], scalar=rw(j),
                                                                                                  in1=xc[:, o0:o0 + n], op0=ALU.mult, op1=ALU.add),
                                 outs=[b_xc], ins=[b_xb, b_const, b_xc])
                    b_xcb = B("rxcb")
                    T.op("pool", lambda e: e.tensor_copy(out=xcb[:], in_=xc[:]), outs=[b_xcb], ins=[b_xc])
                    b_rr, b_ii = B("rr"), B("ri")
                    for which, (dst, bd, boff) in enumerate(((rr, b_rr, PP_BR), (ii, b_ii, PP_BI))):
                        m = (d * 2 + which) * 8 + hd
                        for (t0, nt) in MTT:
                            pi = pk_ % 4
                            pk_ += 1
                            bp_ = B(("rp", pi))
                            T.mm([lambda e, pi=pi, m=m, t0=t0, nt=nt: e.matmul(pr[pi][:, 0:nt], lhsT=wrb[:, m, :], rhs=xcb[:, t0:t0 + nt], start=True, stop=True)],
                                 outs=[bp_], ins=[b_xcb, b_wrb])
                            T.op("act", lambda e, pi=pi, dst=dst, t0=t0, nt=nt, boff=boff: e.activation(out=dst[:, t0:t0 + nt], in_=pr[pi][:, 0:nt], func=AF.Sigmoid,
                                                                                                          bias=pl(boff + dh)), outs=[bd], ins=[bp_, b_const])
                    cl = clam[:, dh:dh + 1]
                    cl2 = clam[:, 16 + dh:16 + dh + 1]
                    b_aa, b_gg, b_hs = B("ra"), B("rg"), B("rhs")
                    T.op("act", lambda e: e.activation(out=aa[:], in_=rr[:], func=AF.Exp, scale=cl), outs=[b_aa], ins=[b_rr, b_const])
                    T.op("act", lambda e: e.activation(out=gg[:], in_=rr[:], func=AF.Exp, scale=cl2), outs=[b_gg], ins=[b_rr, b_const])
                    T.op("dve", lambda e: e.tensor_scalar(out=gg[:], in0=gg[:], scalar1=-1.0, scalar2=1.0, op0=ALU.mult, op1=ALU.add), outs=[b_gg], ins=[b_gg])
                    T.op("dve", lambda e: e.tensor_scalar(out=gg[:], in0=gg[:], scalar1=0.0, scalar2=None, op0=ALU.max), outs=[b_gg], ins=[b_gg])
                    T.op("act", lambda e: e.activation(out=gg[:], in_=gg[:], func=AF.Sqrt), outs=[b_gg], ins=[b_gg])
                    T.op("dve", lambda e: e.tensor_tensor(out=gg[:], in0=gg[:], in1=ii[:], op=ALU.mult), outs=[b_gg], ins=[b_gg, b_ii])
                    T.op("dve", lambda e: e.tensor_tensor(out=gg[:], in0=gg[:], in1=xc[:], op=ALU.mult), outs=[b_gg], ins=[b_gg, b_xc])
                    b_rs = B("rsum")
                    T.op("dve", lambda e: e.reduce_sum(out=rsum[:], in_=rr[:, NCTX:NMIX], axis=AX.X), outs=[b_rs], ins=[b_rr])
                    T.op("act", lambda e, dh=dh: e.activation(out=car[:, dh * 2:dh * 2 + 1], in_=rsum[:], func=AF.Exp, scale=cl), outs=[b_car], ins=[b_rs, b_const])
                    if d == 0:
                        T.op("dve", lambda e: e.tensor_tensor_scan(out=hs[:, 0:NCTX], data0=aa[:, 0:NCTX], data1=gg[:, 0:NCTX], initial=0.0, op0=ALU.mult, op1=ALU.add),
                             outs=[b_hs], ins=[b_aa, b_gg])
                        T.op("dve", lambda e: e.tensor_tensor_scan(out=hs[:, NCTX:NMIX], data0=aa[:, NCTX:NMIX], data1=gg[:, NCTX:NMIX], initial=0.0, op0=ALU.mult, op1=ALU.add),
                             outs=[b_hs], ins=[b_aa, b_gg])
                        T.op("dve", lambda e, hd=hd: e.tensor_copy(out=ybc[:, hd, :], in_=hs[:, 0:NCTX]), outs=[b_ybc], ins=[b_hs])
                        e_c, e_l = NCTX - 1, NMIX - 1
                    else:
                        T.op("dve", lambda e: e.tensor_tensor_scan(out=hs[:, 0:NCTX][:, ::-1], data0=aa[:, 0:NCTX][:, ::-1], data1=gg[:, 0:NCTX][:, ::-1], initial=0.0,
                                                                   op0=ALU.mult, op1=ALU.add), outs=[b_hs], ins=[b_aa, b_gg])
                        T.op("dve", lambda e: e.tensor_tensor_scan(out=hs[:, NCTX:NMIX][:, ::-1], data0=aa[:, NCTX:NMIX][:, ::-1], data1=gg[:, NCTX:NMIX][:, ::-1], initial=0.0,
                                                                   op0=ALU.mult, op1=ALU.add), outs=[b_hs], ins=[b_aa, b_gg])
                        T.op("dve", lambda e, hd=hd: e.tensor_tensor(out=ybc[:, hd, :], in0=ybc[:, hd, :], in1=hs[:, 0:NCTX], op=ALU.add), outs=[b_ybc], ins=[b_hs, b_ybc])
                        e_c, e_l = 0, NCTX
                    T.op("dve", lambda e, dh=dh, e_l=e_l: e.tensor_copy(out=car[:, dh * 2 + 1:dh * 2 + 2], in_=hs[:, e_l:e_l + 1]), outs=[b_car], ins=[b_hs])
                    T.op("dve", lambda e, dh=dh, e_c=e_c: e.tensor_copy(out=ectx[:, dh:dh + 1], in_=hs[:, e_c:e_c + 1]), outs=[b_ectx], ins=[b_hs])
                    if mode != "A":
                        T.dma(SA[dh * 128:(dh + 1) * 128, :], aa[:, NCTX:NMIX], outs=[B(("SA", dh))], ins=[b_aa], q="pool")
                        T.dma(SG[dh * 128:(dh + 1) * 128, :], gg[:, NCTX:NMIX], outs=[B(("SG", dh))], ins=[b_gg], q="pool")
            if mode == "A":
                T.dma(car_out, car[:], outs=[B("out")], ins=[b_car], q="pool")
                T.barrier()
                s3.close()
            else:
                if not last:
                    T.dma(gtc[:], GR.rearrange("(c p) t -> p c t", p=128)[:, :, 0:NCTX], outs=[B("rgtc")], ins=[B(("GR", c)) for c in range(8)])
                    T.op("dve", lambda e: e.tensor_tensor(out=moc[:], in0=ybc[:], in1=gtc[:], op=ALU.mult), outs=[B("rmoc")], ins=[b_ybc, B("rgtc")])
                    T.dma(MIX.rearrange("(c p) t -> p c t", p=128)[:, 8:16, 0:NCTX], moc[:], outs=[B(("MIX", 8 + c)) for c in range(8)], ins=[B("rmoc")], q="pool")
                T.dma(gc_[:], gcard.rearrange("p (r c) -> p r c", c=32), outs=[B("rgc")])
                b_hin, b_tmp = B("hin"), B("rtmp")
                T.op("dve", lambda e: e.tensor_copy(out=hin[:], in_=ectx[:]), outs=[b_hin], ins=[b_ectx])
                for d in range(2):
                    order = range(NC) if d == 0 else range(NC - 1, -1, -1)
                    S = hin[:, d * 8:(d + 1) * 8]
                    for r in order:
                        g3 = gc_[:, r, d * 16:(d + 1) * 16].rearrange("p (h two) -> p h two", two=2)
                        A_r, E_r = g3[:, :, 0], g3[:, :, 1]
                        mcol = flags[:, 2 + d * 8 + r:2 + d * 8 + r + 1]
                        T.op("dve", lambda e, A_r=A_r, S=S: e.scalar_tensor_tensor(out=tmp[:], in0=A_r, scalar=-1.0, in1=S, op0=ALU.add, op1=ALU.mult),
                             outs=[b_tmp], ins=[B("rgc"), b_hin])
                        T.op("dve", lambda e, E_r=E_r: e.tensor_tensor(out=tmp[:], in0=tmp[:], in1=E_r, op=ALU.add), outs=[b_tmp], ins=[b_tmp, B("rgc")])
                        T.op("dve", lambda e, S=S, mcol=mcol: e.scalar_tensor_tensor(out=S, in0=tmp[:], scalar=mcol, in1=S, op0=ALU.mult, op1=ALU.add),
                             outs=[b_hin], ins=[b_tmp, b_hin, b_const])
                T.barrier()
                s3.close()
                a2 = [sb(s2, f"ra2{i}", [128, NLAT]) for i in range(2)]
                g2 = [sb(s2, f"rg2{i}", [128, NLAT]) for i in range(2)]
                hf = sb(s2, "rhf", [128, NLAT])
                hr = sb(s2, "rhr", [128, NLAT])
                gtl = sb(s2, "rgtl", [128, NLAT], BF16)
                mol = sb(s2, "rmol", [128, NLAT], BF16)
                for hd in range(8):
                    for d in range(2):
                        dh = d * 8 + hd
                        ba, bg = B(("ra2", d)), B(("rg2", d))
                        T.dma(a2[d][:], SA[dh * 128:(dh + 1) * 128, :], outs=[ba], ins=[B(("SA", dh))])
                        T.dma(g2[d][:], SG[dh * 128:(dh + 1) * 128, :], outs=[bg], ins=[B(("SG", dh))])
                        if d == 0:
                            T.op("dve", lambda e, dh=dh: e.tensor_tensor_scan(out=hf[:], data0=a2[0][:], data1=g2[0][:], initial=hin[:, dh:dh + 1], op0=ALU.mult, op1=ALU.add),
                                 outs=[B("rhf")], ins=[ba, bg, b_hin])
                        else:
                            T.op("dve", lambda e, dh=dh: e.tensor_tensor_scan(out=hr[:, ::-1], data0=a2[1][:, ::-1], data1=g2[1][:, ::-1], initial=hin[:, dh:dh + 1],
                                                                              op0=ALU.mult, op1=ALU.add), outs=[B("rhr")], ins=[ba, bg, b_hin])
                    T.dma(gtl[:], GR[hd * 128:(hd + 1) * 128, OWN0:OWN0 + NLAT], outs=[B("rgtl")], ins=[B(("GR", hd))])
                    T.op("dve", lambda e: e.tensor_tensor(out=hf[:], in0=hf[:], in1=hr[:], op=ALU.add), outs=[B("rhf")], ins=[B("rhf"), B("rhr")])
                    T.op("dve", lambda e: e.tensor_tensor(out=mol[:], in0=hf[:], in1=gtl[:], op=ALU.mult), outs=[B("rmol")], ins=[B("rhf"), B("rgtl")])
                    T.dma(MIX[(8 + hd) * 128:(9 + hd) * 128, NCTX:NMIX], mol[:], outs=[B(("MIX", 8 + hd))], ins=[B("rmol")], q="pool")
                T.barrier()
        if mode == "A":
            return nc

        with ExitStack() as s1:
            qt = [sb(s1, f"aq{i}", [128, NTK], BF16) for i in range(2)]
            kt = [sb(s1, f"ak{i}", [128, NTK], BF16) for i in range(2)]
            vt = [sb(s1, f"avt{i}", [128, NTK], BF16) for i in range(2)]
            gct = [sb(s1, f"agc{i}", [128, NTK], BF16) for i in range(2)]
            tf = [sb(s1, f"atf{i}", [128, 896]) for i in range(2)]
            vv = sb(s1, "avv", [128, 22, 128], BF16)
            tb = sb(s1, "atb", [128, 5 * 896])
            pp_ = [sb(s1, f"app{i}", [128, 768]) for i in range(2)]
            pT = [sb(s1, f"apT{i}", [128, 1024], BF16) for i in range(2)]
            rc = [sb(s1, f"arc{i}", [128, 128]) for i in range(2)]
            of = [sb(s1, f"aof{i}", [128, 128]) for i in range(2)]
            om = sb(s1, "aom", [128, NMIX], BF16)
            psS = [ps(s1, f"aps{i}", [128, 1024]) for i in range(2)]
            psO = [ps(s1, f"apo{i}", [128, 512]) for i in range(2)]
            psV = [ps(s1, f"apv{i}", [128, 1024], BF16) for i in range(2)]
            scale = 128.0 ** -0.5
            n = 0
            for h in range(16):
                i = h % 2
                bq, bk, bvt, bgc, btf = B(("aq", i)), B(("ak", i)), B(("avt", i)), B(("agc", i)), B(("atf", i))
                T.dma(qt[i][:], QT[h * 128:(h + 1) * 128, :], outs=[bq], ins=[B(("QT", h))])
                T.dma(kt[i][:], KT[h * 128:(h + 1) * 128, :], outs=[bk], ins=[B(("KT", h))])
                T.dma(vt[i][:], VT[h * 128:(h + 1) * 128, :], outs=[bvt], ins=[B(("VT", h))])
                T.dma(gct[i][:], GC[h * 128:(h + 1) * 128, :], outs=[bgc], ins=[B(("GC", h))])
                T.dma(tf[i][:], tfd[h * 128:(h + 1) * 128, :], outs=[btf])
                b_tb = B("atb")
                for v in range(5):
                    T.op("pool", lambda e, v=v, i=i: e.tensor_tensor(out=tb[:, v * 896:(v + 1) * 896], in0=tf[i][:], in1=ms[:, v * 896:(v + 1) * 896], op=ALU.add),
                         outs=[b_tb], ins=[btf, b_const])
                b_vv = B("avv")
                for g in range(0, 22, 8):
                    ng = min(8, 22 - g)
                    j = (g // 8) % 2
                    bpv = B(("apv", j))
                    T.mm([lambda e, g=g, jj=jj, j=j, i=i: e.transpose(psV[j][:, jj * 128:(jj + 1) * 128], vt[i][:, (g + jj) * 128:(g + jj + 1) * 128], identb[:])
                          for jj in range(ng)], outs=[bpv], ins=[bvt, b_const])
                    T.op("act", lambda e, g=g, ng=ng, j=j: e.activation(out=vv[:, g:g + ng, :], in_=psV[j][:, 0:ng * 128].rearrange("p (a b) -> p a b", b=128), func=AF.Copy),
                         outs=[b_vv], ins=[bpv])
                b_om = B("aom")
                blocks = [("c", 0), ("c", 1)] if not last else []
                blocks += [("l", b) for b in range(16)]
                pending = None
                for (bt, b) in blocks:
                    si = n % 2
                    n += 1
                    bS, bO, bpp, bpT, brc, bof = B(("aps", si)), B(("apo", si)), B(("app", si)), B(("apT", si)), B(("arc", si)), B(("aof", si))
                    if bt == "c":
                        q0, m0 = b * 128, b * 128
                        ktiles, var = [], None
                    else:
                        q0, m0 = OWN0 + b * 128, NCTX + b * 128
                        if b == 0:
                            ktiles, var = list(range(0, 6)), 1
                        elif b == 15:
                            ktiles, var = list(range(-1, 5)), 4
                        else:
                            ktiles, var = list(range(0, 5)), {1: 2, 14: 3}.get(b, 0)
                    nl = len(ktiles)
                    def stage1(b=b, q0=q0, si=si, i=i, ktiles=ktiles, var=var, nl=nl, bS=bS, bpp=bpp, bpT=bpT):
                        fns = []
                        for ti_, kt_ in enumerate(ktiles):
                            k0 = NCTX + (b + kt_) * 128
                            fns.append(lambda e, ti_=ti_, k0=k0, q0=q0, si=si, i=i: e.matmul(psS[si][:, ti_ * 128:(ti_ + 1) * 128], lhsT=kt[i][:, k0:k0 + 128], rhs=qt[i][:, q0:q0 + 128],
                                                                                             start=True, stop=True))
                        for cj in range(2):
                            fns.append(lambda e, cj=cj, q0=q0, si=si, i=i: e.matmul(psS[si][:, 768 + cj * 128:896 + cj * 128], lhsT=kt[i][:, cj * 128:(cj + 1) * 128], rhs=qt[i][:, q0:q0 + 128],
                                                                                    start=True, stop=True))
                        T.mm(fns, outs=[bS], ins=[bq, bk])
                        if nl:
                            tcol = var * 896 + ((ktiles[0] + 1) * 128)
                            T.op("dve", lambda e, si=si, nl=nl, tcol=tcol: e.scalar_tensor_tensor(out=pp_[si][:, 0:nl * 128], in0=psS[si][:, 0:nl * 128], scalar=scale,
                                                                                                 in1=tb[:, tcol:tcol + nl * 128], op0=ALU.mult, op1=ALU.add),
                                 outs=[bpp], ins=[bS, b_tb])
                            T.op("act", lambda e, si=si, nl=nl: e.activation(out=pT[si][:, 0:nl * 128], in_=pp_[si][:, 0:nl * 128], func=AF.Exp), outs=[bpT], ins=[bpp])
                        T.op("act", lambda e, si=si: e.activation(out=pT[si][:, 768:1024], in_=psS[si][:, 768:1024], func=AF.Exp, scale=scale), outs=[bpT], ins=[bS])

                    def stage2(b=b, q0=q0, m0=m0, si=si, i=i, ktiles=ktiles, bO=bO, bpT=bpT, brc=brc, bof=bof):
                        tl = [(ti_ * 128, 2 + (b + kt_)) for ti_, kt_ in enumerate(ktiles)] + [(768, 0), (896, 1)]
                        fns = []
                        for m_, (c_, vi) in enumerate(tl):
                            fns.append(lambda e, m_=m_, c_=c_, vi=vi, si=si: e.matmul(psO[si][:, 0:128], lhsT=vv[:, vi, :], rhs=pT[si][:, c_:c_ + 128], start=(m_ == 0), stop=(m_ == len(tl) - 1)))
                        for m_, (c_, vi) in enumerate(tl):
                            fns.append(lambda e, m_=m_, c_=c_, si=si: e.matmul(psO[si][:, 128:256], lhsT=onesb[:], rhs=pT[si][:, c_:c_ + 128], start=(m_ == 0), stop=(m_ == len(tl) - 1)))
                        T.mm(fns, outs=[bO], ins=[bpT, b_vv, b_const])
                        T.op("dve", lambda e, si=si: e.reciprocal(out=rc[si][:], in_=psO[si][:, 128:256]), outs=[brc], ins=[bO])
                        T.op("dve", lambda e, si=si: e.tensor_tensor(out=of[si][:], in0=psO[si][:, 0:128], in1=rc[si][:], op=ALU.mult), outs=[bof], ins=[bO, brc])
                        T.op("pool", lambda e, si=si, q0=q0, m0=m0, i=i: e.tensor_tensor(out=om[:, m0:m0 + 128], in0=of[si][:], in1=gct[i][:, q0:q0 + 128], op=ALU.mult),
                             outs=[b_om], ins=[bof, bgc])

                    stage1()
                    if pending is not None:
                        pending()
                    pending = stage2
                if pending is not None:
                    pending()
                    pending = None
                c0 = NCTX if last else 0
                T.dma(MIX[(16 + h) * 128:(17 + h) * 128, c0:NMIX], om[:, c0:NMIX], outs=[B(("MIX", 16 + h))], ins=[b_om], q="pool")
            T.barrier()

        Xdst = XT if last else xo
        MTT = [(0, 256), (256, 512), (768, 512), (1280, 512), (1792, 512)]
        XCOL = [0, OWN0, OWN0 + 512, OWN0 + 1024, OWN0 + 1536]
        for gi, grp in enumerate([[0, 1, 2], [3, 4]]):
            g0 = MTT[grp[0]][0]
            gn = sum(MTT[t][1] for t in grp)
            with ExitStack() as s1:
                mT = sb(s1, "mT", [128, KC, gn], BF16)
                b_mT = B(("mT", gi))
                T.dma(mT[:], MIX.rearrange("(c p) t -> p c t", p=128)[:, :, g0:g0 + gn], outs=[b_mT], ins=[B(("MIX", c)) for c in range(KC)])
                xo_ = [sb(s1, f"ox{i}", [128, 512]) for i in range(3)]
                ok = {"k": 0}

                def evac_out(chunk, tidx, nt, pt, bps):
                    if last and tidx == 0:
                        return
                    t0 = MTT[tidx][0]
                    i = ok["k"] % 3
                    ok["k"] += 1
                    bx = B(("ox", i))
                    j = 1 if tidx == 0 else 0
                    T.dma(xo_[i][:, 0:nt], xT[chunk * 128:(chunk + 1) * 128, XCOL[tidx]:XCOL[tidx] + nt], outs=[bx])
                    T.op("dve", lambda e: e.scalar_tensor_tensor(out=xo_[i][:, 0:nt], in0=pt[:, 0:nt], scalar=modT[:, j * 96 + 64 + chunk:j * 96 + 65 + chunk], in1=xo_[i][:, 0:nt],
                                                                 op0=ALU.mult, op1=ALU.add), outs=[bx], ins=[bx, bps, b_const])
                    T.dma(Xdst[chunk * 128:(chunk + 1) * 128, t0:t0 + nt], xo_[i][:, 0:nt], outs=[B(("XO", chunk))], ins=[bx], q="pool")

                tiles = [(t, MTT[t][0] - g0, MTT[t][1]) for t in grp]
                gemm(s1, w_out, KC, [c * 256 for c in range(D // 256)], mT, b_mT, tiles, evac_out, "go")
                T.barrier()

        if last:
            with ExitStack() as s1:
                xt = [sb(s1, f"fx{i}", [128, 512]) for i in range(3)]
                sq = [sb(s1, f"fsq{i}", [128, 512]) for i in range(2)]
                rstd = sb(s1, "frstd", [128, 512])
                pss = [ps(s1, f"fps{i}", [128, 512]) for i in range(2)]
                kx = 0
                b_out = B("out")
                for tidx in range(1, 5):
                    t0, nt = MTT[tidx]
                    pst_ = pss[tidx % 2]
                    bpss = B(("fps", tidx % 2))
                    for kc in range(KC):
                        xi = kx % 3
                        kx += 1
                        bxt = B(("fx", xi))
                        T.dma(xt[xi][:], XT[kc * 128:(kc + 1) * 128, t0:t0 + nt], outs=[bxt], ins=[B(("XO", kc))])
                        bsq = B(("fsq", kc % 2))
                        T.op("act", lambda e, xi=xi, kc=kc: e.activation(out=sq[kc % 2][:], in_=xt[xi][:], func=AF.Square), outs=[bsq], ins=[bxt])
                        T.mm([lambda e, kc=kc, pst_=pst_: e.matmul(pst_[:], lhsT=ones[:], rhs=sq[kc % 2][:], start=(kc == 0), stop=(kc == KC - 1))], outs=[bpss], ins=[bsq, b_const])
                    b_rstd = B("frstd")
                    T.op("dve", lambda e, pst_=pst_: e.tensor_scalar(out=rstd[:], in0=pst_[:], scalar1=1.0 / D, scalar2=EPS, op0=ALU.mult, op1=ALU.add), outs=[b_rstd], ins=[bpss])
                    T.op("act", lambda e: e.activation(out=rstd[:], in_=rstd[:], func=AF.Sqrt), outs=[b_rstd], ins=[b_rstd])
                    T.op("dve", lambda e: e.reciprocal(out=rstd[:], in_=rstd[:]), outs=[b_rstd], ins=[b_rstd])
                    for kc in range(KC):
                        xi = kx % 3
                        kx += 1
                        bxt = B(("fx", xi))
                        T.dma(xt[xi][:], XT[kc * 128:(kc + 1) * 128, t0:t0 + nt], outs=[bxt], ins=[B(("XO", kc))])
                        T.op("dve", lambda e, xi=xi, kc=kc: e.scalar_tensor_tensor(out=xt[xi][:], in0=xt[xi][:], scalar=fgt[:, kc:kc + 1], in1=rstd[:], op0=ALU.mult, op1=ALU.mult),
                             outs=[bxt], ins=[bxt, b_rstd, b_const])
                        T.dma(outT[kc * 128:(kc + 1) * 128, t0 - NCTX:t0 - NCTX + nt], xt[xi][:], outs=[b_out], ins=[bxt], q="pool")
                T.barrier()
    return nc


def _ms_tables(me):
    ms = np.full((5, 7, 128, 128), NEG, np.float32)
    for v, b in enumerate((5, 0, 1, 14, 15)):
        for s, ktile in enumerate(range(-1, 6)):
            for a in range(2):
                for c in range(2):
                    key_local = 2 * b + 2 * ktile + a
                    r = 32 * me + 2 * b + c
                    rs = min(max(r - 4, 0), 248)
                    kg = 32 * me + key_local - 4
                    if rs <= kg <= rs + 7:
                        ms[v, s, a * 64:(a + 1) * 64, c * 64:(c + 1) * 64] = 0.0
    return np.ascontiguousarray(ms.transpose(2, 0, 1, 3).reshape(128, 5 * 7 * 128))


def _tf_tables(rpb):
    L, H = rpb.shape[0], rpb.shape[1]
    cols = np.arange(64)
    cs = np.clip(cols - 8, 0, 48)
    kc_, qc_ = np.meshgrid(cols, cols, indexing="ij")
    inwin = (kc_ >= cs[None, :]) & (kc_ < cs[None, :] + 16)
    dc = np.clip(kc_ - qc_ + 15, 0, 30)
    tf = np.full((L, H, 2, 64, 7, 2, 64), NEG, np.float32)
    for s, ktile in enumerate(range(-1, 6)):
        for a in range(2):
            for c in range(2):
                dr = 2 * ktile + a - 4 - c
                if -7 <= dr <= 7:
                    blk = rpb[:, :, dr + 7, :][:, :, dc]
                    tf[:, :, a, :, s, c, :] = np.where(inwin[None, None], blk, NEG)
    return np.ascontiguousarray(tf.reshape(L, H * 128, 7 * 128))


def _fm(v):
    return np.ascontiguousarray(v.reshape(-1, 128).T)


def _pack_params(norm_g, conv_w, conv_b, ln_g, ln_b, b_pw, rconv_w, rconv_b, b_r, b_i, lam):
    pp = np.zeros((DEPTH, 128, NPP), np.float32)
    for l in range(DEPTH):
        pp[l, :, PP_NG:PP_NG + 32] = _fm(norm_g[l])
        pp[l, :, PP_CW:PP_CW + 248] = conv_w[l].reshape(31, 8, 128).transpose(2, 1, 0).reshape(128, 248)
        pp[l, :, PP_CB:PP_CB + 8] = _fm(conv_b[l])
        pp[l, :, PP_LG:PP_LG + 8] = _fm(ln_g[l])
        pp[l, :, PP_LB:PP_LB + 8] = _fm(ln_b[l])
        pp[l, :, PP_BPW:PP_BPW + 8] = _fm(b_pw[l])
        pp[l, :, PP_RW:PP_RW + 64] = rconv_w[l].reshape(2, 4, 8, 128).transpose(3, 0, 2, 1).reshape(128, 64)
        pp[l, :, PP_RB:PP_RB + 16] = rconv_b[l].reshape(16, 128).T
        pp[l, :, PP_BR:PP_BR + 16] = b_r[l].reshape(16, 128).T
        pp[l, :, PP_BI:PP_BI + 16] = b_i[l].reshape(16, 128).T
        pp[l, :, PP_LAM:PP_LAM + 16] = lam[l].reshape(16, 128).T
    return pp


_PROG = {}


def _prog(name):
    if name not in _PROG:
        _PROG[name] = build_mod() if name == "M" else build_layer(name)
    return _PROG[name]


def _launch(name, in_maps):
    return run_bass_kernel_spmd(_prog(name), in_maps, core_ids=list(range(NC))).results


def kernel(x, c, ctx, c_ctx, w_ada, b_ada, norm_g, w_in, conv_w, conv_b, ln_g, ln_b, w_pw, b_pw,
           rconv_w, rconv_b, w_r, b_r, w_i, b_i, lam, rpb, w_out, final_g, _n_layers=DEPTH, _debug=None):
    f = np.float32
    A = lambda v: np.asarray(v, dtype=f)
    x, c, ctx, c_ctx, w_ada, b_ada, w_in, w_out, w_pw = A(x), A(c), A(ctx), A(c_ctx), A(w_ada), A(b_ada), A(w_in), A(w_out), A(w_pw)
    pp = _pack_params(A(norm_g), A(conv_w), A(conv_b), A(ln_g), A(ln_b), A(b_pw), A(rconv_w), A(rconv_b), A(b_r), A(b_i), A(lam))
    w_ri = np.ascontiguousarray(np.stack([A(w_r), A(w_i)], axis=2).reshape(DEPTH, 32 * 128, 128))
    tf = _tf_tables(A(rpb))
    ident = np.eye(128, dtype=f)
    fgt = _fm(A(final_g))
    cvec = np.ascontiguousarray(np.concatenate([_fm(c.reshape(-1)), _fm(c_ctx.reshape(-1))], axis=1))
    res = _launch("M", [{
        "cvec": cvec,
        "w_ada_sh": np.ascontiguousarray(w_ada[:, :, me * ADA_SH:(me + 1) * ADA_SH].reshape(DEPTH * D, ADA_SH)),
        "b_ada_sh": np.ascontiguousarray(np.broadcast_to(b_ada[:, me * ADA_SH:(me + 1) * ADA_SH].reshape(1, -1), (2, DEPTH * ADA_SH))),
    } for me in range(NC)])
    modt = []
    for l in range(DEPTH):
        cols = []
        for j in range(2):
            m = np.concatenate([res[r]["modrow"][j, l * ADA_SH:(l + 1) * ADA_SH] for r in range(NC)])
            cols.append(m.reshape(3, 32, 128).transpose(2, 0, 1).reshape(128, 96))
        modt.append(np.ascontiguousarray(np.concatenate(cols, axis=1).astype(f)))
    flags = []
    for me in range(NC):
        fl = np.zeros((128, 18), f)
        fl[:, 0] = 0.0 if me == 0 else 1.0
        fl[:, 1] = 0.0 if me == NC - 1 else 1.0
        fl[:, 2:2 + me] = 1.0
        fl[:, 10 + me + 1:18] = 1.0
        flags.append(fl)
    mss = [_ms_tables(me) for me in range(NC)]
    own = [np.ascontiguousarray(x[0, me * NLAT:(me + 1) * NLAT, :].T) for me in range(NC)]
    cx = [np.ascontiguousarray(ctx[0].T) for _ in range(NC)]
    out = None
    for l in range(_n_layers):
        last = (l == DEPTH - 1)
        xe = []
        for me in range(NC):
            t = np.zeros((D, NTK), f)
            t[:, 0:NCTX] = cx[me]
            if me > 0:
                t[:, NCTX:OWN0] = own[me - 1][:, NLAT - 256:]
            t[:, OWN0:OWN0 + NLAT] = own[me]
            if me < NC - 1:
                t[:, OWN0 + NLAT:] = own[me + 1][:, :256]
            xe.append(t)
        w_rx = np.ascontiguousarray(w_in[l][:, 3072:4096])
        base = [{"xT": xe[me], "pp": pp[l], "modt": modt[l], "w_ri": w_ri[l], "flags": flags[me]} for me in range(NC)]
        resA = _launch("A", [dict(b, w_rx=w_rx) for b in base])
        gcar = np.ascontiguousarray(np.concatenate([resA[r]["car"] for r in range(NC)], axis=1))
        extra = {"w_in": w_in[l], "w_out": w_out[l], "w_pw": w_pw[l], "tf": tf[l], "gcar": gcar, "ident": ident}
        if last:
            extra["final_g_t"] = fgt
        resB = _launch("L" if last else "B", [dict(base[me], ms=mss[me], **extra) for me in range(NC)])
        if _debug is not None:
            _debug.append((resA, resB))
        if last:
            out = np.empty((1, NC * NLAT, D), f)
            for me in range(NC):
                out[0, me * NLAT:(me + 1) * NLAT, :] = resB[me]["outT"].T
        else:
            own = [np.ascontiguousarray(resB[me]["xo"][:, NCTX:]) for me in range(NC)]
            cx = [np.ascontiguousarray(resB[me]["xo"][:, :NCTX]) for me in range(NC)]
    return out
```
